# Optimizing a Trainium2 kernel written in Bass

```python
import jax, jax.numpy as jnp
from jax import lax
import numpy as np

D_MODEL = 2048
BATCH = 4
SEQ = 2048
DEPTH = 2

N_MIXERS = 2
N_A_LAYERS = (DEPTH + 1) // 2
N_B_LAYERS = DEPTH // 2
BRANCH_WIDTH = D_MODEL
SGU_CHUNK = 128
SGU_GROUPS = 16
SGU_GROUP_WIDTH = BRANCH_WIDTH // SGU_GROUPS
MOBA_HEADS = 16
MOBA_HEAD_DIM = BRANCH_WIDTH // MOBA_HEADS
MOBA_BLOCK = 256
MOBA_TOP_K = 3
MOBA_Q_CHUNK = 16
PLE_DIM = 256
LN_EPS = 1e-5
DEEPNORM_ALPHA = (2 * DEPTH) ** 0.25
DEEPNORM_BETA = (8 * DEPTH) ** -0.25

kernel_name = "hybrid_sgu_moba_deepnorm_ple"


def layer_norm(x, g, b):
    xf = x.astype(jnp.float32)
    mu = jnp.mean(xf, axis=-1, keepdims=True)
    var = jnp.mean(jnp.square(xf - mu), axis=-1, keepdims=True)
    y = (xf - mu) * lax.rsqrt(var + LN_EPS)
    return (y * g.astype(jnp.float32) + b.astype(jnp.float32)).astype(x.dtype)


def sgu_mixer(x, w_in, v_g, v_b, w_s, b_s):
    B, S, _ = x.shape
    W = BRANCH_WIDTH
    h = x @ w_in
    uv = jax.nn.gelu(h[..., :2 * W])
    z = h[..., 2 * W:]
    u, v = uv[..., :W], uv[..., W:]
    v = layer_norm(v, v_g, v_b)
    nc = S // SGU_CHUNK
    v = v.reshape(B, nc, SGU_CHUNK, SGU_GROUPS, SGU_GROUP_WIDTH)
    causal = jnp.tril(jnp.ones((SGU_CHUNK, SGU_CHUNK), dtype=bool))
    w_causal = jnp.where(causal[None], w_s, jnp.zeros_like(w_s))
    s = jnp.einsum('gts,bnsgc->bntgc', w_causal, v) + b_s.T[None, None, :, :, None]
    s = s.reshape(B, S, W)
    return u * s * jax.nn.silu(z)


def moba_mixer(x, w_in):
    B, S, _ = x.shape
    H, hd, BLK, QC = MOBA_HEADS, MOBA_HEAD_DIM, MOBA_BLOCK, MOBA_Q_CHUNK
    h = x @ w_in
    q, k, v, z = jnp.split(h, 4, axis=-1)

    def heads(t):
        return t.reshape(B, S, H, hd).transpose(0, 2, 1, 3)

    q = heads(q) * (hd ** -0.5)
    k, v = heads(k), heads(v)
    nb = -(-S // BLK)
    pad = ((0, 0), (0, 0), (0, nb * BLK - S), (0, 0))
    k_blocks = jnp.pad(k, pad).reshape(B, H, nb, BLK, hd)
    v_blocks = jnp.pad(v, pad).reshape(B, H, nb, BLK, hd)
    k_mean = jnp.mean(k_blocks, axis=3)
    n_sel = min(MOBA_TOP_K, nb)
    blk_ids = jnp.arange(nb)
    b_ix = jnp.arange(B)[:, None, None, None]
    h_ix = jnp.arange(H)[None, :, None, None]
    neg_inf = jnp.float32(-jnp.inf)

    def chunk(c):
        q0 = c * QC
        qc = lax.dynamic_slice_in_dim(q, q0, QC, axis=2)
        own = q0 // BLK
        gate = jnp.einsum('bhqd,bhnd->bhqn', qc, k_mean).astype(jnp.float32)
        gate = jnp.where((blk_ids < own)[None, None, None, :], gate, neg_inf)
        _, idx = lax.top_k(gate, n_sel)
        slot_valid = jnp.repeat(jnp.arange(n_sel) < own, BLK)
        k_sel = k_blocks[b_ix, h_ix, idx].reshape(B, H, QC, n_sel * BLK, hd)
        v_sel = v_blocks[b_ix, h_ix, idx].reshape(B, H, QC, n_sel * BLK, hd)
        s_sel = jnp.einsum('bhqd,bhqkd->bhqk', qc, k_sel).astype(jnp.float32)
        s_sel = jnp.where(slot_valid[None, None, None, :], s_sel, neg_inf)
        k_own = lax.dynamic_slice_in_dim(k_blocks, own, 1, axis=2)[:, :, 0]
        v_own = lax.dynamic_slice_in_dim(v_blocks, own, 1, axis=2)[:, :, 0]
        s_own = jnp.einsum('bhqd,bhkd->bhqk', qc, k_own).astype(jnp.float32)
        q_pos = q0 + jnp.arange(QC)
        k_pos = own * BLK + jnp.arange(BLK)
        s_own = jnp.where((k_pos[None, :] <= q_pos[:, None])[None, None], s_own, neg_inf)
        probs = jax.nn.softmax(jnp.concatenate([s_sel, s_own], axis=-1), axis=-1).astype(v.dtype)
        o = jnp.einsum('bhqk,bhqkd->bhqd', probs[..., :n_sel * BLK], v_sel)
        o = o + jnp.einsum('bhqk,bhkd->bhqd', probs[..., n_sel * BLK:], v_own)
        return o

    out = lax.map(chunk, jnp.arange(S // QC))
    out = out.transpose(1, 0, 3, 2, 4).reshape(B, S, H * hd)
    return out * jax.nn.silu(z)


def setup_inputs(seed: int = 0) -> dict:
    key = jax.random.key(seed)
    ks = jax.random.split(key, 14)
    D, W = D_MODEL, BRANCH_WIDTH
    f32 = jnp.float32
    x = jax.random.normal(ks[0], (BATCH, SEQ, D), f32)
    p = jax.random.normal(ks[1], (DEPTH, BATCH, SEQ, PLE_DIM), f32)
    w_in_a = jax.random.normal(ks[2], (N_A_LAYERS, D, 3 * W), f32) * D ** -0.5
    sgu_norm_g = 1.0 + 0.05 * jax.random.normal(ks[3], (N_A_LAYERS, W), f32)
    sgu_norm_b = 0.05 * jax.random.normal(ks[4], (N_A_LAYERS, W), f32)
    w_s = jax.random.normal(ks[5], (N_A_LAYERS, SGU_GROUPS, SGU_CHUNK, SGU_CHUNK), f32) * SGU_CHUNK ** -0.5
    b_s = 1.0 + 0.1 * jax.random.normal(ks[6], (N_A_LAYERS, SGU_GROUPS, SGU_CHUNK), f32)
    w_in_b = jax.random.normal(ks[7], (N_B_LAYERS, D, 4 * W), f32) * D ** -0.5
    w_out = jax.random.normal(ks[8], (DEPTH, W, D), f32) * (W ** -0.5 * DEEPNORM_BETA)
    ln_g = 1.0 + 0.05 * jax.random.normal(ks[9], (DEPTH, D), f32)
    ln_b = 0.05 * jax.random.normal(ks[10], (DEPTH, D), f32)
    w_ple_gate = jax.random.normal(ks[11], (DEPTH, D, D), f32) * D ** -0.5
    w_ple_proj = jax.random.normal(ks[12], (DEPTH, PLE_DIM, D), f32) * PLE_DIM ** -0.5
    return {"x": x, "p": p, "w_in_a": w_in_a, "sgu_norm_g": sgu_norm_g,
            "sgu_norm_b": sgu_norm_b, "w_s": w_s, "b_s": b_s, "w_in_b": w_in_b,
            "w_out": w_out, "ln_g": ln_g, "ln_b": ln_b,
            "w_ple_gate": w_ple_gate, "w_ple_proj": w_ple_proj}


def reference(x, p, w_in_a, sgu_norm_g, sgu_norm_b, w_s, b_s, w_in_b,
              w_out, ln_g, ln_b, w_ple_gate, w_ple_proj):
    for i in range(DEPTH):
        j = i // N_MIXERS
        if i % N_MIXERS == 0:
            y = sgu_mixer(x, w_in_a[j], sgu_norm_g[j], sgu_norm_b[j], w_s[j], b_s[j])
        else:
            y = moba_mixer(x, w_in_b[j])
        y = y @ w_out[i]
        x = layer_norm(DEEPNORM_ALPHA * x + y, ln_g[i], ln_b[i])
        x = x + jax.nn.sigmoid(x @ w_ple_gate[i]) * (p[i] @ w_ple_proj[i])
    return x
```

```python
import os
import numpy as np
from contextlib import ExitStack
import concourse.bass as bass
import concourse.mybir as mybir
from concourse.bass_utils import run_bass_kernel_spmd

F32 = mybir.dt.float32
BF16 = mybir.dt.bfloat16
AF = mybir.ActivationFunctionType
ALU = mybir.AluOpType
AX = mybir.AxisListType

NCORES = 8
P = 128
NT = 8
TOK = 1024
D = 2048
KT = 16
NEG = -30000.0
ALPHA = 4.0 ** 0.25
EPS = 1e-5
QSCALE = 128.0 ** -0.5
PAIRS = [[0, 1], [2, 3], [4, 5], [6, 7]]

C_M01 = 0
C_CM0 = 128
C_CM1 = 256
C_ID = 384
C_FB = 512
C_NS = 544
NCST = 576


class DSem:
    def __init__(self, sem):
        self.sem = sem
        self.cnt = 0


class Queue:
    def __init__(self, name, sem):
        self.name = name
        self.sem = sem
        self.cnt = 0
        self.ops = []
        self.waited = {}

    def wait(self, *toks):
        for t in toks:
            if t is None:
                continue
            if isinstance(t, (list,)):
                self.wait(*t)
                continue
            sem, val = t
            k = id(sem)
            if self.waited.get(k, 0) >= val:
                continue
            self.waited[k] = val
            self.ops.append(("w", sem, val))

    def do(self, fn, sig=True):
        tok = None
        if sig:
            self.cnt += 1
            tok = (self.sem, self.cnt)
        self.ops.append(("d", fn, tok))
        return tok

    def dma(self, ds, fn):
        ds.cnt += 16
        self.ops.append(("dma", fn, ds.sem))
        return (ds.sem, ds.cnt)

    def cc(self, ds, fn):
        ds.cnt += 1
        self.ops.append(("cc", fn, ds.sem))
        return (ds.sem, ds.cnt)

    def emit(self, eng):
        for op in self.ops:
            if op[0] == "w":
                eng.wait_ge(op[1], op[2])
            elif op[0] == "d":
                ins = op[1](eng)
                if op[2] is not None:
                    ins.then_inc(op[2][0], 1)
            elif op[0] == "dma":
                op[1](eng).then_inc(op[2], 16)
            else:
                op[1](eng).then_inc(op[2])


def _check_deadlock(queues):
    val = {}
    ptr = [0] * len(queues)
    while True:
        prog = False
        for qi, q in enumerate(queues):
            while ptr[qi] < len(q.ops):
                op = q.ops[ptr[qi]]
                if op[0] == "w":
                    if val.get(id(op[1]), 0) < op[2]:
                        break
                elif op[0] == "d":
                    if op[2] is not None:
                        val[id(op[2][0])] = val.get(id(op[2][0]), 0) + 1
                elif op[0] == "dma":
                    val[id(op[2])] = val.get(id(op[2]), 0) + 16
                else:
                    val[id(op[2])] = val.get(id(op[2]), 0) + 1
                ptr[qi] += 1
                prog = True
        if all(ptr[i] == len(q.ops) for i, q in enumerate(queues)):
            return
        if not prog:
            msg = []
            for qi, q in enumerate(queues):
                if ptr[qi] < len(q.ops):
                    op = q.ops[ptr[qi]]
                    owner = [qq.name for qq in queues if qq.sem is op[1]]
                    msg.append(f"{q.name}@{ptr[qi]}/{len(q.ops)} waits {owner or 'dma'} >= {op[2]} (now {val.get(id(op[1]), 0)})")
            raise RuntimeError("DEADLOCK: " + "; ".join(msg))


def build_nc(dbg=None):
    nc = bass.Bass("TRN2", target_bir_lowering=False)
    es = ExitStack()

    early = dbg in ("P0", "V", "UZ", "S")
    skip = {"P0": ("w_in_a", "w_in_b", "w_out", "w_gate", "w_proj"), "V": ("w_in_b", "w_out", "w_gate", "w_proj"),
            "UZ": ("w_in_b", "w_out", "w_gate", "w_proj"), "S": ("w_in_b", "w_out", "w_gate", "w_proj"),
            "L0": ("w_in_b",), "O1": ("w_in_b",), "O2": ("w_in_b",)}.get(dbg, ())

    def din(name, shape):
        if name in skip:
            return nc.dram_tensor(name, list(shape), F32).ap()
        return nc.dram_tensor(name, list(shape), F32, kind="ExternalInput").ap()

    x_in = din("x", [TOK, D])
    xT_in = din("xT", [D, TOK])
    pT_in = din("pT", [2, 256, TOK])
    w_in_a = din("w_in_a", [D, 3 * D])
    w_in_b = din("w_in_b", [D, 4 * D])
    w_out = din("w_out", [2, D, D])
    w_gate = din("w_gate", [2, D, D])
    w_proj = din("w_proj", [2, 256, D])
    sgu_g = din("sgu_g", [1, D])
    sgu_b = din("sgu_b", [1, D])
    wsT_in = din("wsT", [P, 16, P])
    bs_in = din("bs", [1, D])
    ln_g = din("ln_g", [2, D])
    ln_b = din("ln_b", [2, D])
    cst_in = din("cst", [P, NCST])
    cste_in = din("cste", [P, 1024])
    out = nc.dram_tensor("out", [TOK, D], F32, kind="ExternalOutput").ap()

    x1sp = nc.dram_tensor("x1sp", [TOK, D], F32).ap()
    mineK = [nc.dram_tensor(f"mineK{c}", [1024, TOK], BF16) for c in range(2)]
    allK = [nc.dram_tensor(f"allK{c}", [2048, TOK], BF16) for c in range(2)]
    mineV = [nc.dram_tensor(f"mineV{c}", [TOK, 1024], BF16) for c in range(2)]
    allV = [nc.dram_tensor(f"allV{c}", [2 * TOK, 1024], BF16) for c in range(2)]

    arena = es.enter_context(nc.sbuf_tensor("arena", [P, 212480 // 4], F32))
    base = nc.lookup_mloc(arena).addr
    K = 1024
    R_A, R_B, R_C, R_D, R_E, R_S = 0, 32 * K, 96 * K, 128 * K, 160 * K, 192 * K

    def sb(name, shape, dt, off):
        return nc.alloc_sbuf_tensor_at(name, list(shape), dt, offset=base + off)

    xT = sb("xT", [P, KT, TOK], BF16, R_A)
    xres = sb("xres", [P, NT, D], F32, R_B)
    wbB = [sb(f"wbB{i}", [P, KT, 512], BF16, R_B + i * 16 * K) for i in range(4)]
    mT = sb("mT", [P, KT, TOK], BF16, R_C)
    zT = sb("zT", [P, KT, TOK], BF16, R_D)
    gb = sb("gb", [P, D], F32, R_D)
    bb = sb("bb", [P, D], F32, R_D + 8 * K)
    vtmp = [sb(f"vtmp{i}", [P, D], F32, R_D + 16 * K + i * 8 * K) for i in range(2)]
    tmpb = sb("tmpb", [P, D], BF16, R_D + 16 * K)
    sgt = [sb(f"sgt{i}", [P, 512], F32, R_D + 20 * K + i * 2 * K) for i in range(2)]
    wpb = sb("wpb", [P, 2, D], BF16, R_D + 24 * K)
    vn = sb("vn", [P, NT, D], BF16, R_E)
    wbE = [sb(f"wbE{i}", [P, KT, 512], BF16, R_E + i * 16 * K) for i in range(2)]
    wsb = sb("wsb", [P, 16, P], BF16, R_A)
    bf2 = sb("bf2", [2, D], F32, R_A + 4 * K)
    blo = sb("blo", [2, D], F32, R_A + 12 * K)
    btA = sb("btA", [2, D], F32, R_A + 20 * K)
    bhi = sb("bhi", [2, D], BF16, R_A + 28 * K)
    brow = sb("brow", [2, D], BF16, R_A + 4 * K)
    kst = [sb(f"kst{i}", [P, TOK], BF16, R_B + i * 2 * K) for i in range(2)]
    vst = [sb(f"vst{i}", [P, 512], BF16, R_B + 4 * K + i * K) for i in range(4)]
    kTh = [sb(f"kTh{i}", [P, 2, TOK], BF16, R_A + i * 4 * K) for i in range(2)]
    vh = [sb(f"vh{i}", [P, 2, NT, P], BF16, R_A + 8 * K + i * 4 * K) for i in range(2)]
    NPT = 6
    PT = [sb(f"PT{i}", [P, 256], BF16, R_A + 25 * K + i * 512) for i in range(NPT)]
    rden = [sb(f"rden{i}", [P, 256], F32, R_A + 18 * K + i * K) for i in range(2)]
    otmp = [sb(f"otmp{i}", [P, 256], F32, R_A + 20 * K + i * K) for i in range(2)]
    km32 = sb("km32", [P, 8], F32, R_A + 22 * K)
    kmb = sb("kmb", [P, 8], BF16, R_A + 22 * K + 64)
    gm = sb("gm", [P, 4, 8], F32, R_A + 22 * K + 128)
    top8 = sb("top8", [P, 4, 8], F32, R_A + 22 * K + 256)
    selb = sb("selb", [P, 4, 8], F32, R_A + 22 * K + 384)
    selbb = sb("selbb", [P, 4, 8], BF16, R_A + 22 * K + 512)
    selT = [sb(f"selT{i}", [P, 512], BF16, R_A + 23 * K + i * K) for i in range(2)]
    o = R_S
    ident = sb("ident", [P, P], BF16, o); o += 256
    ones = sb("ones", [P, P], BF16, o); o += 256
    cmb = sb("cmb", [P, 2, P], BF16, o); o += 512
    m01 = sb("m01", [P, P], BF16, o); o += 256
    eoh = sb("eoh", [P, 8, P], BF16, o); o += 2048
    pTb = sb("pTb", [P, 2, TOK], BF16, o); o += 4096
    cstf = sb("cstf", [P, 64], F32, o); o += 256
    stats_l = [sb(f"stats{i}", [P, 4, 6], F32, o + i * 96) for i in range(2)]; o += 192
    mv_l = [sb(f"mv{i}", [P, 2], F32, o + i * 32) for i in range(2)]; o += 64
    sd_l = [sb(f"sd{i}", [P, 1], F32, o + i * 32) for i in range(2)]; o += 64
    rstd_l = [sb(f"rstd{i}", [P, 1], F32, o + i * 32) for i in range(2)]; o += 64
    nb_l = [sb(f"nb{i}", [P, 1], F32, o + i * 32) for i in range(2)]; o += 64
    assert o <= 212480 - 64

    ps = [es.enter_context(nc.psum_tensor(f"ps{i}", [P, 512], F32)) for i in range(6)]
    pstl = [es.enter_context(nc.psum_tensor(f"pst{i}", [P, 1024], BF16)) for i in range(2)]

    nsem = [0]

    def newsem(name):
        nsem[0] += 1
        return es.enter_context(nc.semaphore(f"{name}_{nsem[0]}"))

    PE = Queue("pe", newsem("pe"))
    ACT = Queue("act", newsem("act"))
    DVE = Queue("dve", newsem("dve"))
    POOL = Queue("pool", newsem("pool"))
    SP = Queue("sp", newsem("sp"))

    def dsem(name):
        return DSem(newsem(name))

    bank_free = [None] * 4
    bank_i = [0]

    def next_bank():
        b = bank_i[0] % 4
        bank_i[0] += 1
        return b

    def wsrc(w_ap, col0, ncol=512, kt=KT):
        return w_ap.rearrange("(kt p) n -> p kt n", p=P)[:, :, col0:col0 + ncol]

    class WStream:
        def __init__(self, bufs, name):
            self.bufs = bufs
            self.free = [None] * len(bufs)
            self.sems = [dsem(f"{name}{i}") for i in range(len(bufs))]
            self.i = 0

        def load(self, src, extra_wait=None):
            b = self.i % len(self.bufs)
            self.i += 1
            POOL.wait(self.free[b], extra_wait)
            buf = self.bufs[b]
            tok = POOL.dma(self.sems[b], lambda e, buf=buf, src=src: e.dma_start(out=buf[:, :, :], in_=src))
            return b, tok

    def mm_group(out_ap, pairs, first_waits=()):
        n = len(pairs)
        PE.wait(*first_waits)
        tok = None
        for k, (l, r) in enumerate(pairs):
            tok = PE.do(lambda e, l=l, r=r, k=k: e.matmul(out_ap, l, r, start=(k == 0), stop=(k == n - 1)),
                        sig=(k == n - 1))
        return tok

    ln_cnt = [0]
    ln_war = [None, None]

    def ln_tile(src, free_waits):
        k = ln_cnt[0] % 2
        ln_cnt[0] += 1
        stats, mv, sd, rstd = stats_l[k], mv_l[k], sd_l[k], rstd_l[k]
        DVE.wait(*free_waits)
        DVE.wait(ln_war[k])
        for c in range(4):
            t = DVE.do(lambda e, c=c: e.bn_stats(out=stats[:, c, :], in_=src[:, c * 512:(c + 1) * 512]))
        DVE.wait(t)
        t = DVE.do(lambda e: e.bn_aggr(out=mv[:, :], in_=stats[:, :, :].rearrange("p a b -> p (a b)")))
        ACT.wait(t)
        t = ACT.do(lambda e: e.activation(out=sd[:, :], in_=mv[:, 1:2], func=AF.Sqrt, bias=EPS, scale=1.0))
        DVE.wait(t)
        t = DVE.do(lambda e: e.reciprocal(out=rstd[:, :], in_=sd[:, :]))
        DVE.wait(t)
        return mv, rstd, k

    def transposes_to(dstT, i, src_bf, src_tok, war_tok, evq=None):
        toks = []
        for g4 in range(4):
            half = g4 % 2
            PE.wait(src_tok, tr_free[half])
            for j in range(4):
                kt = g4 * 4 + j
                t = PE.do(lambda e, kt=kt, j=j, half=half: e.transpose(
                    out=pstl[half][:, j * 128:(j + 1) * 128],
                    in_=src_bf[:, kt * 128:(kt + 1) * 128], identity=ident[:, :]), sig=(j == 3))
            if evq is ACT:
                ACT.wait(t, war_tok)
                t2 = ACT.do(lambda e, g4=g4, half=half: e.activation(
                    out=dstT[:, g4 * 4:(g4 + 1) * 4, i * 128:(i + 1) * 128],
                    in_=pstl[half][:, 0:512].rearrange("p (a b) -> p a b", a=4), func=AF.Copy))
            else:
                DVE.wait(t, war_tok)
                t2 = DVE.do(lambda e, g4=g4, half=half: e.tensor_copy(
                    out=dstT[:, g4 * 4:(g4 + 1) * 4, i * 128:(i + 1) * 128],
                    in_=pstl[half][:, 0:512].rearrange("p (a b) -> p a b", a=4)))
            tr_free[half] = t2
            toks.append(t2)
        return toks

    tr_free = [None, None]

    class _Done(Exception):
        pass

    fin = {}

    def dump(srcs, tok):
        toks = []
        s_d = dsem("dbgdump")
        for i, src in enumerate(srcs):
            ACT.wait(tok)
            t = ACT.do(lambda e, i=i, src=src: e.activation(out=xres[:, i, :], in_=src, func=AF.Copy))
            SP.wait(t)
            toks.append(SP.dma(s_d, lambda e, i=i: e.dma_start(out=out[i * 128:(i + 1) * 128, :], in_=xres[:, i, :])))
        fin["toks"] = toks
        raise _Done()

    def phases():
        s_cst = dsem("cst")
        t_cst = [
            POOL.dma(s_cst, lambda e: e.dma_start(out=m01[:, :], in_=cst_in[:, C_M01:C_M01 + 128])),
            POOL.dma(s_cst, lambda e: e.dma_start(out=cmb[:, 0, :], in_=cst_in[:, C_CM0:C_CM0 + 128])),
            POOL.dma(s_cst, lambda e: e.dma_start(out=cmb[:, 1, :], in_=cst_in[:, C_CM1:C_CM1 + 128])),
            POOL.dma(s_cst, lambda e: e.dma_start(out=ident[:, :], in_=cst_in[:, C_ID:C_ID + 128])),
            POOL.dma(s_cst, lambda e: e.dma_start(out=eoh[:, :, :].rearrange("p a b -> p (a b)"), in_=cste_in[:, :])),
            POOL.dma(s_cst, lambda e: e.dma_start(out=cstf[:, :], in_=cst_in[:, C_FB:C_FB + 64])),
        ][-1]
        t_ones = DVE.do(lambda e: e.memset(ones[:, :], 1.0))

        s_xT = [dsem(f"xT{i}") for i in range(4)]
        xTv = xT_in.rearrange("(kt p) t -> p kt t", p=P)
        def xT_load(q):
            return POOL.dma(s_xT[q], lambda e, q=q: e.dma_start(out=xT[:, :, q * 256:(q + 1) * 256],
                                                               in_=xTv[:, :, q * 256:(q + 1) * 256]))
        t_xT = [xT_load(0), None, None, None]
        wsB = WStream(wbB, "wB")
        wsE = WStream(wbE, "wE")
        if dbg == "P0":
            for q in range(1, 4):
                t_xT[q] = xT_load(q)
            dump([xT[:, 2 * i:2 * i + 2, :].rearrange("p a b -> p (a b)") for i in range(NT)], t_xT + [t_cst])

        wv_tok = []
        for c in range(4):
            b, t = wsB.load(wsrc(w_in_a, D + c * 512))
            wv_tok.append(t)
        for q in range(1, 4):
            t_xT[q] = xT_load(q)
        s_par = dsem("par")
        t_par = [SP.dma(s_par, lambda e: e.dma_start(out=gb[:, :], in_=sgu_g.to_broadcast([P, D]))),
                 SP.dma(s_par, lambda e: e.dma_start(out=bb[:, :], in_=sgu_b.to_broadcast([P, D])))][-1]

        vtmp_free = [None, None]
        last_v_mm = [None] * 4
        pre_act = {0: [], 1: []}

        def v_unit(i, c):
            vt = vtmp[i % 2]
            b = next_bank()
            pairs = [(xT[:, kt, i * 128:(i + 1) * 128], wbB[c][:, kt, :]) for kt in range(KT)]
            tmm = mm_group(ps[b][:, :], pairs, [bank_free[b], t_xT[i // 2], wv_tok[c]])
            last_v_mm[c] = tmm
            ACT.wait(tmm, vtmp_free[i % 2])
            ta = ACT.do(lambda e, b=b, c=c, vt=vt: e.activation(out=vt[:, c * 512:(c + 1) * 512], in_=ps[b][:, :],
                                                              func=AF.Gelu_apprx_tanh))
            bank_free[b] = ta
            return ta

        for c in range(4):
            for i in (0, 1):
                pre_act[i].append(v_unit(i, c))
        for i in range(NT):
            vt = vtmp[i % 2]
            if i < 2:
                act_toks = pre_act[i]
            else:
                act_toks = [v_unit(i, c) for c in range(4)]
            mv, rstd, lk = ln_tile(vt, act_toks)
            DVE.wait(t_par)
            t = DVE.do(lambda e, vt=vt, mv=mv: e.scalar_tensor_tensor(out=vt[:, :], in0=vt[:, :], scalar=mv[:, 0:1],
                                                                    in1=gb[:, :], op0=ALU.subtract, op1=ALU.mult))
            DVE.wait(t)
            t = DVE.do(lambda e, vt=vt, i=i, rstd=rstd: e.scalar_tensor_tensor(
                out=vn[:, i, :], in0=vt[:, :], scalar=rstd[:, 0:1], in1=bb[:, :], op0=ALU.mult, op1=ALU.add))
            ln_war[lk] = t
            vtmp_free[i % 2] = t
        t_vn_done = t
        for c in range(4):
            wsB.free[c] = last_v_mm[c]
        if dbg == "V":
            dump([vn[:, i, :] for i in range(NT)], [t_vn_done, last_v_mm])

        chunks = [("u", c) for c in range(4)] + [("z", c) for c in range(4)]
        loads = {}

        def uz_load(n):
            kind, c = chunks[n]
            col = c * 512 if kind == "u" else 2 * D + c * 512
            loads[n] = wsB.load(wsrc(w_in_a, col))

        LA = 3
        for n in range(min(LA, len(chunks))):
            uz_load(n)
        t_z_war = t_vn_done
        last_act_uz = None
        for n, (kind, c) in enumerate(chunks):
            if n + LA < len(chunks):
                uz_load(n + LA)
            wb_i, wtok = loads[n]
            dst = mT if kind == "u" else zT
            fn = AF.Gelu_apprx_tanh if kind == "u" else AF.Silu
            for ct in range(4):
                ctile = c * 4 + ct
                for half in range(2):
                    b = next_bank()
                    pairs = [(wbB[wb_i][:, kt, ct * 128:(ct + 1) * 128], xT[:, kt, half * 512:(half + 1) * 512])
                             for kt in range(KT)]
                    tmm = mm_group(ps[b][:, :], pairs, [bank_free[b], wtok, t_xT[2 * half], t_xT[2 * half + 1]])
                    ACT.wait(tmm, t_z_war if kind == "z" else None)
                    ta = ACT.do(lambda e, b=b, dst=dst, ctile=ctile, half=half, fn=fn: e.activation(
                        out=dst[:, ctile, half * 512:(half + 1) * 512], in_=ps[b][:, :], func=fn))
                    bank_free[b] = ta
                    last_act_uz = ta
                if kind == "z":
                    DVE.wait(ta)
                    last_act_uz = DVE.do(lambda e, ctile=ctile: e.tensor_tensor(
                        out=mT[:, ctile, :], in0=mT[:, ctile, :], in1=zT[:, ctile, :], op=ALU.mult))
            wsB.free[wb_i] = tmm
        t_uz_mm_done = tmm
        t_uz_done = last_act_uz
        if dbg == "UZ":
            dump([mT[:, 2 * i:2 * i + 2, :].rearrange("p a b -> p (a b)") for i in range(NT)],
                 [t_uz_done, t_uz_mm_done])

        s_s = dsem("sgu")
        POOL.wait(t_uz_mm_done)
        t_ws = POOL.dma(s_s, lambda e: e.dma_start(out=wsb[:, :, :], in_=wsT_in[:, :, :]))
        s_s2 = dsem("sgub")
        SP.wait(t_uz_mm_done)
        SP.dma(s_s2, lambda e: e.dma_start(out=bf2[0:1, :], in_=bs_in[:, :]))
        t_bs = SP.dma(s_s2, lambda e: e.dma_start(out=bf2[1:2, :], in_=bs_in[:, :]))
        DVE.wait(t_bs, t_cst)
        t = DVE.do(lambda e: e.tensor_copy(out=bhi[:, :], in_=bf2[:, :]))
        DVE.wait(t)
        t = DVE.do(lambda e: e.tensor_tensor(out=blo[:, :], in0=bf2[:, :], in1=bhi[:, :], op=ALU.subtract))
        t = DVE.do(lambda e: e.tensor_scalar(out=btA[:, :], in0=bhi[:, :], scalar1=ident[0:2, 0:1], scalar2=None,
                                             op0=ALU.mult))
        DVE.wait(t)
        t_brow = DVE.do(lambda e: e.scalar_tensor_tensor(out=brow[:, :], in0=blo[:, :], scalar=ident[0:2, 1:2],
                                                         in1=btA[:, :], op0=ALU.mult, op1=ALU.add))
        DVE.wait(t_ws)
        for g in range(16):
            t = DVE.do(lambda e, g=g: e.tensor_tensor(out=wsb[:, g, :], in0=wsb[:, g, :], in1=m01[:, :], op=ALU.mult),
                       sig=(g == 15))
        t_wsb = t
        t_s_last = None
        for g in range(16):
            for nh in range(2):
                b = next_bank()
                PE.wait(bank_free[b], t_wsb, t_vn_done, t_brow, t_ones)
                for j in range(4):
                    n = nh * 4 + j
                    PE.do(lambda e, b=b, j=j, n=n, g=g: e.matmul(
                        ps[b][:, j * 128:(j + 1) * 128], vn[:, n, g * 128:(g + 1) * 128], wsb[:, g, :],
                        start=True, stop=False, skip_group_check=True), sig=False)
                    tmm = PE.do(lambda e, b=b, j=j, g=g: e.matmul(
                        ps[b][:, j * 128:(j + 1) * 128], ones[0:2, :], brow[0:2, g * 128:(g + 1) * 128],
                        start=False, stop=True, skip_group_check=True), sig=(j == 3))
                DVE.wait(tmm, last_act_uz)
                t = DVE.do(lambda e, b=b, g=g, nh=nh: e.tensor_tensor(
                    out=mT[:, g, nh * 512:(nh + 1) * 512], in0=ps[b][:, :], in1=mT[:, g, nh * 512:(nh + 1) * 512],
                    op=ALU.mult))
                bank_free[b] = t
                t_s_last = t
        t_s_mm_done = tmm
        if dbg == "S":
            dump([mT[:, 2 * i:2 * i + 2, :].rearrange("p a b -> p (a b)") for i in range(NT)],
                 [t_s_last, t_s_mm_done])

        def x_out(l, xsrc, t_m_ready, t_rd_free, t_re_free, t_rb_free, t_ra_free, final):
            s_x = [dsem(f"xr{l}_{i}") for i in range(NT)]
            SP.wait(t_rb_free)
            t_x = [SP.dma(s_x[i], lambda e, i=i: e.dma_start(out=xres[:, i, :], in_=xsrc[i * 128:(i + 1) * 128, :]))
                   for i in range(NT)]
            s_p = dsem(f"par{l}")
            SP.wait(t_rd_free)
            SP.dma(s_p, lambda e: e.dma_start(out=gb[:, :], in_=ln_g[l:l + 1, :].to_broadcast([P, D])))
            t_par = SP.dma(s_p, lambda e: e.dma_start(out=bb[:, :], in_=ln_b[l:l + 1, :].to_broadcast([P, D])))
            wl = [wsrc(w_out[l], c * 512) for c in range(4)] + [wsrc(w_gate[l], c * 512) for c in range(4)]
            ld = {}
            ld[0] = wsE.load(wl[0], t_re_free)
            ld[1] = wsE.load(wl[1], t_re_free)
            s_pp = dsem(f"pp{l}")
            POOL.wait(t_rd_free)
            POOL.dma(s_pp, lambda e: e.dma_start(out=pTb[:, :, :], in_=pT_in[l].rearrange("(k p) t -> p k t", p=P)))
            t_pp = POOL.dma(s_pp, lambda e: e.dma_start(out=wpb[:, :, :],
                                                       in_=w_proj[l].rearrange("(k p) n -> p k n", p=P)))
            r_tok = [[None] * 4 for _ in range(NT)]
            xT_tok = [None] * NT
            st = {"tmpb_free": None}
            lnv = [None] * NT
            t1b = [None] * NT
            t_nb = [None] * NT

            def s1a(i):
                lnv[i] = ln_tile(xres[:, i, :], r_tok[i])
                mv, rstd, lk = lnv[i]
                nb = nb_l[lk]
                t_nb[i] = DVE.do(lambda e, mv=mv, rstd=rstd, nb=nb: e.tensor_scalar(
                    out=nb[:, :], in0=mv[:, 0:1], scalar1=rstd[:, 0:1], scalar2=-1.0, op0=ALU.mult, op1=ALU.mult))

            def s1b(i):
                xi = xres[:, i, :]
                mv, rstd, lk = lnv[i]
                nb = nb_l[lk]
                ACT.wait(t_nb[i])
                t = ACT.do(lambda e, xi=xi, rstd=rstd, nb=nb: e.activation(out=xi, in_=xi, func=AF.Identity,
                                                                         bias=nb[:, 0:1], scale=rstd[:, 0:1]))
                ln_war[lk] = t
                DVE.wait(t, t_par)
                t1b[i] = DVE.do(lambda e, xi=xi: e.tensor_tensor(out=xi, in0=xi, in1=gb[:, :], op=ALU.mult))

            def s2(i):
                xi = xres[:, i, :]
                POOL.wait(t1b[i], t_par)
                t = POOL.do(lambda e, xi=xi: e.tensor_tensor(out=xi, in0=xi, in1=bb[:, :], op=ALU.add))
                ACT.wait(t, st["tmpb_free"])
                tc = ACT.do(lambda e, xi=xi: e.activation(out=tmpb[:, :], in_=xi, func=AF.Copy))
                toks = transposes_to(xT, i, tmpb, tc, t_ra_free, evq=ACT)
                st["tmpb_free"] = toks[-1]
                xT_tok[i] = toks

            ln_step = [0]

            def ln_advance():
                step = ln_step[0]
                ln_step[0] += 1
                if step < NT:
                    s1a(step)
                if 1 <= step <= NT:
                    s1b(step - 1)
                if step >= 2:
                    s2(step - 2)

            for c in range(4):
                wb_i, wtok = ld[c]
                for i in range(NT):
                    b = next_bank()
                    pairs = [(mT[:, kt, i * 128:(i + 1) * 128], wbE[wb_i][:, kt, :]) for kt in range(KT)]
                    tmm = mm_group(ps[b][:, :], pairs, [bank_free[b], wtok, t_m_ready])
                    DVE.wait(tmm, t_x[i])
                    t = DVE.do(lambda e, b=b, i=i, c=c: e.scalar_tensor_tensor(
                        out=xres[:, i, c * 512:(c + 1) * 512], in0=xres[:, i, c * 512:(c + 1) * 512], scalar=ALPHA,
                        in1=ps[b][:, :], op0=ALU.mult, op1=ALU.add))
                    bank_free[b] = t
                    r_tok[i][c] = t
                    if c == 3:
                        ln_advance()
                wsE.free[wb_i] = tmm
                if c + 2 < 8:
                    ld[c + 2] = wsE.load(wl[c + 2])
            if dbg == "O1":
                s_d = dsem("dbgo1")
                SP.wait([r_tok[i] for i in range(NT)])
                fin["toks"] = [SP.dma(s_d, lambda e, i=i: e.dma_start(out=out[i * 128:(i + 1) * 128, :],
                                                                    in_=xres[:, i, :])) for i in range(NT)]
                raise _Done()
            while ln_step[0] < NT + 2:
                ln_advance()
            tmpb_free = st["tmpb_free"]
            if dbg == "O2":
                dump([xT[:, 2 * i:2 * i + 2, :].rearrange("p a b -> p (a b)") for i in range(NT)],
                     [xT_tok[i] for i in range(NT)])
            fin_tok = [None] * NT
            sg_free = [None, None]
            kk = 0
            out_sems = dsem(f"out{l}")
            deferred = None
            res = {}

            def finish_tile(i):
                if final:
                    SP.wait(fin_tok[i])
                    res.setdefault("out", []).append(
                        SP.dma(out_sems, lambda e, i=i: e.dma_start(out=out[i * 128:(i + 1) * 128, :], in_=xres[:, i, :])))
                else:
                    SP.wait(fin_tok[i])
                    res.setdefault("out", []).append(
                        SP.dma(out_sems, lambda e, i=i: e.dma_start(out=x1sp[i * 128:(i + 1) * 128, :], in_=xres[:, i, :])))
                    ACT.wait(fin_tok[i], res.get("tmpb_free"))
                    tc = ACT.do(lambda e, i=i: e.activation(out=tmpb[:, :], in_=xres[:, i, :], func=AF.Copy))
                    toks = transposes_to(xT, i, tmpb, tc, res["gate_mm_last"][i], evq=ACT)
                    res["tmpb_free"] = toks[-1]
                    res.setdefault("x1T", [None] * NT)[i] = toks

            res["gate_mm_last"] = [None] * NT
            res["tmpb_free"] = tmpb_free
            for c in range(4):
                wb_i, wtok = ld[4 + c]
                for i in range(NT):
                    bg = next_bank()
                    pairs = [(xT[:, kt, i * 128:(i + 1) * 128], wbE[wb_i][:, kt, :]) for kt in range(KT)]
                    tg = mm_group(ps[bg][:, :], pairs, [bank_free[bg], wtok] + xT_tok[i])
                    if c == 3:
                        res["gate_mm_last"][i] = tg
                    bp = next_bank()
                    pairs = [(pTb[:, k2, i * 128:(i + 1) * 128], wpb[:, k2, c * 512:(c + 1) * 512]) for k2 in range(2)]
                    tp = mm_group(ps[bp][:, :], pairs, [bank_free[bp], t_pp])
                    sg = sgt[kk % 2]
                    ACT.wait(tg, sg_free[kk % 2])
                    ta = ACT.do(lambda e, bg=bg, sg=sg: e.activation(out=sg[:, :], in_=ps[bg][:, :], func=AF.Sigmoid))
                    bank_free[bg] = ta
                    DVE.wait(ta, tp)
                    t = DVE.do(lambda e, bp=bp, sg=sg: e.tensor_tensor(out=sg[:, :], in0=sg[:, :], in1=ps[bp][:, :],
                                                                     op=ALU.mult))
                    bank_free[bp] = t
                    DVE.wait(t)
                    t = DVE.do(lambda e, i=i, c=c, sg=sg: e.tensor_tensor(
                        out=xres[:, i, c * 512:(c + 1) * 512], in0=xres[:, i, c * 512:(c + 1) * 512], in1=sg[:, :],
                        op=ALU.add))
                    sg_free[kk % 2] = t
                    kk += 1
                    if c == 3:
                        fin_tok[i] = t
                        if deferred is not None:
                            finish_tile(deferred)
                        deferred = i
                wsE.free[wb_i] = tg
                if 4 + c + 2 < 8:
                    ld[4 + c + 2] = wsE.load(wl[4 + c + 2])
            finish_tile(deferred)
            res["last_dve"] = t
            res["last_gate_mm"] = tg
            return res

        r0 = x_out(0, x_in, [t_s_last], [t_s_last], [t_s_mm_done], [t_uz_mm_done], [t_s_last],
                   final=(dbg in ("L0", "O1", "O2")))

        final_out_toks = r0["out"]
        if dbg not in ("L0", "O1", "O2"):
            x1T_tok = r0["x1T"]
            t_rb_free = r0["out"]
            kload = {}
            kv_chunks = [("k", c) for c in range(4)] + [("v", c) for c in range(4)] + \
                        [("q", c) for c in range(4)] + [("z", c) for c in range(4)]

            wsL = WStream([wbE[0], wbE[1], wbB[1], wbB[2], wbB[3]], "wL")
            wsL.free = [wsE.free[0], wsE.free[1], t_rb_free, t_rb_free, t_rb_free]
            LAK = 4

            def kv_load(n):
                kind, c = kv_chunks[n]
                col = {"q": 0, "k": D, "v": 2 * D, "z": 3 * D}[kind] + c * 512
                kload[n] = wsL.load(wsrc(w_in_b, col))

            for n in range(LAK):
                kv_load(n)
            s_kst = [dsem(f"kst{i}") for i in range(2)]
            s_vst = [dsem(f"vst{i}") for i in range(4)]
            kst_free = [None, None]
            vst_free = [None] * 4
            s_cc = [dsem(f"cc{i}") for i in range(4)]
            cc_tok = {}
            kstores = [[], []]
            vstores = [[], []]
            kslot = 0
            vslot = 0
            t_qz_war = r0["last_dve"]
            for n, (kind, c) in enumerate(kv_chunks):
                wb_i, wtok = kload[n]
                if kind in ("k", "q", "z"):
                    for hh in range(4):
                        h = c * 4 + hh
                        for half in range(2):
                            b = next_bank()
                            pairs = [(wsL.bufs[wb_i][:, kt, hh * 128:(hh + 1) * 128],
                                      xT[:, kt, half * 512:(half + 1) * 512]) for kt in range(KT)]
                            fw = [bank_free[b], wtok]
                            for i in range(half * 4, half * 4 + 4):
                                fw += x1T_tok[i]
                            tmm = mm_group(ps[b][:, :], pairs, fw)
                            if kind == "k":
                                sl = kslot % 2
                                ACT.wait(tmm, kst_free[sl], t_rb_free if kslot < 2 else None)
                                ta = ACT.do(lambda e, b=b, sl=sl, half=half: e.activation(
                                    out=kst[sl][:, half * 512:(half + 1) * 512], in_=ps[b][:, :], func=AF.Copy))
                            elif kind == "q":
                                ACT.wait(tmm, t_qz_war)
                                ta = ACT.do(lambda e, b=b, h=h, half=half: e.activation(
                                    out=zT[:, h, half * 512:(half + 1) * 512], in_=ps[b][:, :], func=AF.Copy,
                                    scale=QSCALE))
                            else:
                                ACT.wait(tmm, t_qz_war, r0["last_gate_mm"])
                                ta = ACT.do(lambda e, b=b, h=h, half=half: e.activation(
                                    out=mT[:, h, half * 512:(half + 1) * 512], in_=ps[b][:, :], func=AF.Silu))
                            bank_free[b] = ta
                        if kind == "k":
                            sl = kslot % 2
                            SP.wait(ta)
                            ts = SP.dma(s_kst[sl], lambda e, sl=sl, h=h: e.dma_start(
                                out=mineK[h // 8][(h % 8) * 128:(h % 8 + 1) * 128, :], in_=kst[sl][:, :]))
                            kst_free[sl] = ts
                            kstores[h // 8].append(ts)
                            kslot += 1
                    if kind == "k" and c % 2 == 1:
                        cidx = c // 2
                        POOL.wait(kstores[cidx])
                        cc_tok[("k", cidx)] = POOL.cc(s_cc[cidx], lambda e, cidx=cidx: e.collective_compute(
                            "AllGather", ALU.bypass, replica_groups=PAIRS,
                            ins=[mineK[cidx].ap().opt()], outs=[allK[cidx].ap().opt()]))
                else:
                    for i in range(NT):
                        b = next_bank()
                        pairs = [(xT[:, kt, i * 128:(i + 1) * 128], wsL.bufs[wb_i][:, kt, :]) for kt in range(KT)]
                        tmm = mm_group(ps[b][:, :], pairs, [bank_free[b], wtok] + x1T_tok[i])
                        sl = vslot % 4
                        ACT.wait(tmm, vst_free[sl])
                        ta = ACT.do(lambda e, b=b, sl=sl: e.activation(out=vst[sl][:, :], in_=ps[b][:, :], func=AF.Copy))
                        bank_free[b] = ta
                        SP.wait(ta)
                        ts = SP.dma(s_vst[sl], lambda e, sl=sl, i=i, c=c: e.dma_start(
                            out=mineV[c // 2][i * 128:(i + 1) * 128, (c % 2) * 512:(c % 2 + 1) * 512], in_=vst[sl][:, :]))
                        vst_free[sl] = ts
                        vstores[c // 2].append(ts)
                        vslot += 1
                    if c % 2 == 1:
                        cidx = c // 2
                        POOL.wait(vstores[cidx])
                        cc_tok[("v", cidx)] = POOL.cc(s_cc[2 + cidx], lambda e, cidx=cidx: e.collective_compute(
                            "AllGather", ALU.bypass, replica_groups=PAIRS,
                            ins=[mineV[cidx].ap().opt()], outs=[allV[cidx].ap().opt()]))
                wsL.free[wb_i] = tmm
                if n + LAK < len(kv_chunks):
                    kv_load(n + LAK)
            wsE.free = [wsL.free[0], wsL.free[1]]
            t_qz_mm_done = tmm
            t_qz_done = ta
            if dbg == "Q":
                SP.wait([cc_tok[k] for k in cc_tok])
                dump([zT[:, 2 * i:2 * i + 2, :].rearrange("p a b -> p (a b)") for i in range(NT)],
                     [t_qz_done, t_qz_mm_done] + r0["out"])

            s_kv = [dsem(f"kvh{i}") for i in range(2)]
            kv_free = [None, None]
            kv_tok = {}

            def att_load(h):
                sl = h % 2
                cidx = h // 8
                hh = h % 8
                SP.wait(kv_free[sl], cc_tok[("k", cidx)], cc_tok[("v", cidx)], t_qz_mm_done)
                ksrc = allK[cidx].ap().rearrange("(r hh d) t -> d r hh t", r=2, hh=8)[:, :, hh, :]
                SP.dma(s_kv[sl], lambda e, sl=sl, ksrc=ksrc: e.dma_start(out=kTh[sl][:, :, :], in_=ksrc))
                t = None
                for r in range(2):
                    vsrc = allV[cidx].ap().rearrange("(r i p) (hh d) -> p r i hh d", r=2, i=NT, hh=8)[:, r, :, hh, :]
                    t = SP.dma(s_kv[sl], lambda e, sl=sl, vsrc=vsrc, r=r: e.dma_start(out=vh[sl][:, r, :, :], in_=vsrc))
                kv_tok[h] = t

            att_load(0)
            DVE.wait(t_qz_mm_done)
            DVE.do(lambda e: e.memset(selT[0][:, :], 0.0))
            DVE.do(lambda e: e.memset(selT[1][:, :], 0.0))
            S = [ps[2][:, 0:256], ps[3][:, 0:256], ps[4][:, 0:256], ps[5][:, 0:256]]
            OA = [ps[0][:, 0:256], ps[1][:, 0:256]]
            DA = [ps[0][:, 256:512], ps[1][:, 256:512]]
            psg = pstl[1].bitcast(F32)
            s_free = [[bank_free[2], bank_free[3]]] * 4
            pt_free = [None] * NPT
            acc_free = [bank_free[0], bank_free[1]]
            acc_free = [[bank_free[0], bank_free[1], bank_free[2], bank_free[3]]] * 2
            selT_free = [None, None]
            gidx = 0
            aidx = 0
            t_att_dve = None
            t_kmb_h = {}
            t_selbb_h = {}
            t_sel_h = {}

            def prepA_dve(hp):
                kTp = kTh[hp % 2]
                DVE.wait(kv_tok[hp])
                t = DVE.do(lambda e, kTp=kTp: e.tensor_reduce(
                    out=km32[:, :], in_=kTp[:, :, :].rearrange("p r (j t) -> p j r t", j=NT), axis=AX.XY, op=ALU.add))
                DVE.wait(t)
                t_kmb_h[hp] = DVE.do(lambda e: e.tensor_scalar(out=kmb[:, :], in0=km32[:, :], scalar1=1.0 / 256.0,
                                                               scalar2=None, op0=ALU.mult))

            def prepA_pe(hp):
                PE.wait(t_kmb_h[hp], t_qz_done, tr_free[1])
                for ii in range(4):
                    i = 4 + ii
                    tg = PE.do(lambda e, ii=ii, i=i, hp=hp: e.matmul(psg[:, ii * 8:(ii + 1) * 8],
                                                                    zT[:, hp, i * 128:(i + 1) * 128], kmb[:, :],
                                                                    start=True, stop=True), sig=(ii == 3))
                DVE.wait(tg, t_cst)
                t = DVE.do(lambda e: e.tensor_tensor(out=gm[:, :, :].rearrange("p a b -> p (a b)"), in0=psg[:, 0:32],
                                                     in1=cstf[:, 0:32], op=ALU.add))
                DVE.wait(t)
                for ii in range(4):
                    t = DVE.do(lambda e, ii=ii: e.max(out=top8[:, ii, :], in_=gm[:, ii, :]))
                DVE.wait(t)
                for ii in range(4):
                    t = DVE.do(lambda e, ii=ii: e.tensor_scalar(out=selb[:, ii, :], in0=gm[:, ii, :],
                                                               scalar1=top8[:, ii, 2:3], scalar2=1.0e12,
                                                               op0=ALU.subtract, op1=ALU.mult))
                DVE.wait(t)
                t = DVE.do(lambda e: e.tensor_scalar(out=selb[:, :, :].rearrange("p a b -> p (a b)"),
                                                     in0=selb[:, :, :].rearrange("p a b -> p (a b)"),
                                                     scalar1=0.0, scalar2=None, op0=ALU.min))
                DVE.wait(t)
                t_selbb_h[hp] = DVE.do(lambda e: e.tensor_tensor(out=selbb[:, :, :].rearrange("p a b -> p (a b)"),
                                                                 in0=selb[:, :, :].rearrange("p a b -> p (a b)"),
                                                                 in1=cstf[:, 32:64], op=ALU.mult))

            def prepB(hp):
                sTp = selT[hp % 2]
                PE.wait(t_selbb_h[hp], tr_free[0])
                for ii in range(4):
                    tt = PE.do(lambda e, ii=ii: e.transpose(out=pstl[0][0:8, ii * 128:(ii + 1) * 128],
                                                           in_=selbb[:, ii, :], identity=ident[:, :]), sig=(ii == 3))
                DVE.wait(tt, selT_free[hp % 2])
                t_sel_h[hp] = DVE.do(lambda e, sTp=sTp: e.tensor_copy(out=sTp[0:8, :], in_=pstl[0][0:8, 0:512]))
                tr_free[0] = t_sel_h[hp]

            for h in range(16):
                sl = h % 2
                if h + 1 < 16:
                    att_load(h + 1)
                kT_ = kTh[sl]
                v_ = vh[sl]
                sT = selT[sl]
                if h == 0:
                    prepA_dve(0)
                    prepA_pe(0)
                    prepB(0)
                t_sel = t_sel_h[h]
                for m in range(4):
                    if h + 1 < 16:
                        if m == 1:
                            prepA_dve(h + 1)
                        elif m == 2:
                            prepA_pe(h + 1)
                        elif m == 3:
                            prepB(h + 1)
                    a = 0 if os.environ.get('KDBG_A0') else aidx % 2
                    aidx += 1
                    tiles = []
                    for j in range(2 * m + 2):
                        for t_ in range(2):
                            c0 = 0 if j <= 2 * m else 128
                            tiles.append((j, t_, c0))
                    ntile = len(tiles)
                    ex_tok = [None] * ntile
                    slots = [None] * ntile
                    q0 = m * 256

                    def emit_qk(n):
                        nonlocal gidx
                        j, t_, c0 = tiles[n]
                        s = gidx % 4
                        pslot = gidx % NPT
                        gidx += 1
                        slots[n] = pslot
                        ncol = 256 - c0
                        so = S[s][:, c0:256]
                        mms = [(kT_[:, t_, j * 128:(j + 1) * 128], zT[:, h, q0 + c0:q0 + 256], so)]
                        if m >= 2 and j < 2 * m + 1:
                            mms.append((eoh[:, j, :], sT[:, (m - 2) * 256 + c0:(m - 2) * 256 + 256], so))
                        if j == 2 * m:
                            mms.append((ident[:, :], cmb[:, t_, :], S[s][:, 0:128]))
                        if j == 2 * m + 1:
                            mms.append((ident[:, :], cmb[:, t_, :], S[s][:, 128:256]))
                        PE.wait(s_free[s], t_sel if m >= 2 else None, t_cst)
                        tk = None
                        for k, (l, r, oap) in enumerate(mms):
                            tk = PE.do(lambda e, l=l, r=r, oap=oap, k=k, last=(k == len(mms) - 1): e.matmul(
                                oap, l, r, start=(k == 0), stop=last, skip_group_check=True), sig=(k == len(mms) - 1))
                        ACT.wait(tk, pt_free[pslot])
                        ex_tok[n] = ACT.do(lambda e, s=s, pslot=pslot, c0=c0: e.activation(
                            out=PT[pslot][:, c0:256], in_=S[s][:, c0:256], func=AF.Exp))
                        s_free[s] = ex_tok[n]

                    def emit_pv(n):
                        j, t_, c0 = tiles[n]
                        s = slots[n]
                        PE.wait(ex_tok[n], acc_free[a] if n == 0 else None, t_ones)
                        PE.do(lambda e, s=s, c0=c0, j=j, t_=t_, n=n, a=a, v_=v_, ntile=ntile: e.matmul(
                            OA[a][:, c0:256], v_[:, t_, j, :], PT[s][:, c0:256], start=(n == 0), stop=(n == ntile - 1),
                            skip_group_check=True), sig=False)
                        tk = PE.do(lambda e, s=s, c0=c0, n=n, a=a, ntile=ntile: e.matmul(
                            DA[a][:, c0:256], ones[:, :], PT[s][:, c0:256], start=False, stop=(n == ntile - 1),
                            skip_group_check=True))
                        pt_free[s] = tk
                        return tk

                    LAQ = int(os.environ.get('KDBG_LAQ', '3'))
                    tk_last = None
                    for n in range(ntile):
                        emit_qk(n)
                        if n >= LAQ:
                            tk_last = emit_pv(n - LAQ)
                    for n in range(max(0, ntile - LAQ), ntile):
                        tk_last = emit_pv(n)
                    rd, ot = rden[a], otmp[a]
                    DVE.wait(tk_last)
                    t = DVE.do(lambda e, a=a, rd=rd: e.reciprocal(out=rd[:, :], in_=DA[a]))
                    DVE.wait(t)
                    t = DVE.do(lambda e, a=a, rd=rd, ot=ot: e.tensor_tensor(out=ot[:, :], in0=OA[a], in1=rd[:, :],
                                                                          op=ALU.mult))
                    acc_free[a] = t
                    DVE.wait(t)
                    t = DVE.do(lambda e, h=h, q0=q0, ot=ot: e.tensor_tensor(
                        out=mT[:, h, q0:q0 + 256], in0=ot[:, :], in1=mT[:, h, q0:q0 + 256], op=ALU.mult))
                    t_att_dve = t
                    if dbg == "A2" or (dbg == "A3" and m == 3) or (dbg == "A4" and m == 1) or (dbg == "A5" and m == 2):
                        SP.wait(kv_tok[h + 1])
                        dump([mT[:, 2 * i:2 * i + 2, :].rearrange("p a b -> p (a b)") for i in range(NT)],
                             [t_att_dve, tk_last] + r0["out"])
                kv_free[sl] = tk_last
                selT_free[sl] = tk_last
            bank_free[0] = bank_free[1] = t_att_dve
            if dbg == "ATT":
                dump([mT[:, 2 * i:2 * i + 2, :].rearrange("p a b -> p (a b)") for i in range(NT)],
                     [t_att_dve, tk_last] + r0["out"])
            bank_free[2] = bank_free[3] = s_free[0:4]
            tr_free[1] = t_att_dve
            t_att_pe = tk_last

            r1 = x_out(1, x1sp, [t_att_dve], [t_att_pe, t_att_dve], [t_qz_mm_done], [t_att_pe] + r0["out"], [t_att_pe, t_att_dve],
                       final=True)
            final_out_toks = r1["out"]

        fin["toks"] = final_out_toks

    try:
        phases()
    except _Done:
        pass
    final_out_toks = fin["toks"]
    SP.wait(final_out_toks)
    for q in (PE, ACT, DVE, POOL):
        pass

    _check_deadlock([PE, ACT, DVE, POOL, SP])

    with nc.Block() as block:
        @block.sync
        def _(e):
            SP.emit(e)

        @block.gpsimd
        def _(e):
            POOL.emit(e)

        @block.scalar
        def _(e):
            ACT.emit(e)

        @block.vector
        def _(e):
            DVE.emit(e)

        @block.tensor
        def _(e):
            PE.emit(e)
    es.close()
    return nc


def make_consts(rank):
    cst = np.zeros((P, NCST), np.float32)
    s = np.arange(P)[:, None]
    t = np.arange(P)[None, :]
    cst[:, C_M01:C_M01 + 128] = (t >= s).astype(np.float32)
    tri = np.where(s <= t, 0.0, NEG).astype(np.float32)
    if rank == 0:
        cst[:, C_CM0:C_CM0 + 128] = tri
        cst[:, C_CM1:C_CM1 + 128] = NEG
    else:
        cst[:, C_CM0:C_CM0 + 128] = 0.0
        cst[:, C_CM1:C_CM1 + 128] = tri
    cst[:, C_ID:C_ID + 128] = np.eye(P, dtype=np.float32)
    for ii in range(4):
        for j in range(8):
            cst[:, C_FB + ii * 8 + j] = NEG if j >= ii + 4 else 0.0
            cst[:, C_NS + ii * 8 + j] = 1.0 if j < ii + 4 else 0.0
    cste = np.zeros((P, 8, P), np.float32)
    for j in range(8):
        cste[j, j, :] = 1.0
    return cst, cste.reshape(P, 1024)


def tok_index(rank):
    return np.concatenate([np.arange(i * 256 + rank * 128, i * 256 + rank * 128 + 128) for i in range(NT)])


_NC_CACHE = {}


def kernel(x, p, w_in_a, sgu_norm_g, sgu_norm_b, w_s, b_s, w_in_b, w_out, ln_g, ln_b, w_ple_gate, w_ple_proj,
           _dbg=None):
    f = lambda a: np.ascontiguousarray(np.asarray(a, dtype=np.float32))
    x, p = f(x), f(p)
    shared = {
        "w_in_a": f(w_in_a[0]), "w_in_b": f(w_in_b[0]), "w_out": f(w_out), "w_gate": f(w_ple_gate),
        "w_proj": f(w_ple_proj), "sgu_g": f(sgu_norm_g[0:1]), "sgu_b": f(sgu_norm_b[0:1]),
        "wsT": f(np.transpose(np.asarray(w_s[0]), (2, 0, 1))), "bs": f(np.asarray(b_s[0]).reshape(1, D)),
        "ln_g": f(ln_g), "ln_b": f(ln_b),
    }
    in_maps = []
    for c in range(NCORES):
        b, r = c // 2, c % 2
        idx = tok_index(r)
        xc = x[b][idx]
        cst, cste = make_consts(r)
        m = dict(shared)
        m["x"] = f(xc)
        m["xT"] = f(xc.T)
        m["pT"] = f(np.stack([p[l, b][idx].T for l in range(2)]))
        m["cst"] = cst
        m["cste"] = cste
        in_maps.append(m)
    if _dbg not in _NC_CACHE:
        _NC_CACHE[_dbg] = build_nc(_dbg)
    nc = _NC_CACHE[_dbg]
    res = run_bass_kernel_spmd(nc, in_maps, core_ids=list(range(NCORES)))
    outp = np.empty((4, 2048, D), np.float32)
    for c in range(NCORES):
        b, r = c // 2, c % 2
        outp[b][tok_index(r)] = res.results[c]["out"]
    return outp
```

```python
import os
import numpy as np
from contextlib import ExitStack
import concourse.bass as bass
import concourse.mybir as mybir
from concourse.bass_utils import run_bass_kernel_spmd

F32 = mybir.dt.float32
BF16 = mybir.dt.bfloat16
AF = mybir.ActivationFunctionType
ALU = mybir.AluOpType
AX = mybir.AxisListType

NCORES = 8
P = 128
NT = 8
TOK = 1024
D = 2048
KT = 16
NEG = -30000.0
ALPHA = 4.0 ** 0.25
EPS = 1e-5
QSCALE = 128.0 ** -0.5
PAIRS = [[0, 1], [2, 3], [4, 5], [6, 7]]

C_M01 = 0
C_CM0 = 128
C_CM1 = 256
C_ID = 384
C_FB = 512
C_NS = 544
NCST = 576


class DSem:
    def __init__(self, sem):
        self.sem = sem
        self.cnt = 0


class Queue:
    def __init__(self, name, sem):
        self.name = name
        self.sem = sem
        self.cnt = 0
        self.ops = []
        self.waited = {}

    def wait(self, *toks):
        for t in toks:
            if t is None:
                continue
            if isinstance(t, (list,)):
                self.wait(*t)
                continue
            sem, val = t
            k = id(sem)
            if self.waited.get(k, 0) >= val:
                continue
            self.waited[k] = val
            self.ops.append(("w", sem, val))

    def do(self, fn, sig=True):
        tok = None
        if sig:
            self.cnt += 1
            tok = (self.sem, self.cnt)
        self.ops.append(("d", fn, tok))
        return tok

    def dma(self, ds, fn):
        ds.cnt += 16
        self.ops.append(("dma", fn, ds.sem))
        return (ds.sem, ds.cnt)

    def cc(self, ds, fn):
        ds.cnt += 1
        self.ops.append(("cc", fn, ds.sem))
        return (ds.sem, ds.cnt)

    def emit(self, eng):
        for op in self.ops:
            if op[0] == "w":
                eng.wait_ge(op[1], op[2])
            elif op[0] == "d":
                ins = op[1](eng)
                if op[2] is not None:
                    ins.then_inc(op[2][0], 1)
            elif op[0] == "dma":
                op[1](eng).then_inc(op[2], 16)
            else:
                op[1](eng).then_inc(op[2])


def _check_deadlock(queues):
    val = {}
    ptr = [0] * len(queues)
    while True:
        prog = False
        for qi, q in enumerate(queues):
            while ptr[qi] < len(q.ops):
                op = q.ops[ptr[qi]]
                if op[0] == "w":
                    if val.get(id(op[1]), 0) < op[2]:
                        break
                elif op[0] == "d":
                    if op[2] is not None:
                        val[id(op[2][0])] = val.get(id(op[2][0]), 0) + 1
                elif op[0] == "dma":
                    val[id(op[2])] = val.get(id(op[2]), 0) + 16
                else:
                    val[id(op[2])] = val.get(id(op[2]), 0) + 1
                ptr[qi] += 1
                prog = True
        if all(ptr[i] == len(q.ops) for i, q in enumerate(queues)):
            return
        if not prog:
            msg = []
            for qi, q in enumerate(queues):
                if ptr[qi] < len(q.ops):
                    op = q.ops[ptr[qi]]
                    owner = [qq.name for qq in queues if qq.sem is op[1]]
                    msg.append(f"{q.name}@{ptr[qi]}/{len(q.ops)} waits {owner or 'dma'} >= {op[2]} (now {val.get(id(op[1]), 0)})")
            raise RuntimeError("DEADLOCK: " + "; ".join(msg))


def build_nc(dbg=None):
    nc = bass.Bass("TRN2", target_bir_lowering=False)
    es = ExitStack()

    early = dbg in ("P0", "V", "UZ", "S")
    skip = {"P0": ("w_in_a", "w_in_b", "w_out", "w_gate", "w_proj"), "V": ("w_in_b", "w_out", "w_gate", "w_proj"),
            "UZ": ("w_in_b", "w_out", "w_gate", "w_proj"), "S": ("w_in_b", "w_out", "w_gate", "w_proj"),
            "L0": ("w_in_b",), "O1": ("w_in_b",), "O2": ("w_in_b",)}.get(dbg, ())

    def din(name, shape):
        if name in skip:
            return nc.dram_tensor(name, list(shape), F32).ap()
        return nc.dram_tensor(name, list(shape), F32, kind="ExternalInput").ap()

    x_in = din("x", [TOK, D])
    xT_in = din("xT", [D, TOK])
    pT_in = din("pT", [2, 256, TOK])
    w_in_a = din("w_in_a", [D, 3 * D])
    w_in_b = din("w_in_b", [D, 4 * D])
    w_out = din("w_out", [2, D, D])
    w_gate = din("w_gate", [2, D, D])
    w_proj = din("w_proj", [2, 256, D])
    sgu_g = din("sgu_g", [1, D])
    sgu_b = din("sgu_b", [1, D])
    wsT_in = din("wsT", [P, 16, P])
    bs_in = din("bs", [1, D])
    ln_g = din("ln_g", [2, D])
    ln_b = din("ln_b", [2, D])
    cst_in = din("cst", [P, NCST])
    cste_in = din("cste", [P, 1024])
    out = nc.dram_tensor("out", [TOK, D], F32, kind="ExternalOutput").ap()

    x1sp = nc.dram_tensor("x1sp", [TOK, D], F32).ap()
    mineK = [nc.dram_tensor(f"mineK{c}", [1024, TOK], BF16) for c in range(2)]
    allK = [nc.dram_tensor(f"allK{c}", [2048, TOK], BF16) for c in range(2)]
    mineV = [nc.dram_tensor(f"mineV{c}", [TOK, 1024], BF16) for c in range(2)]
    allV = [nc.dram_tensor(f"allV{c}", [2 * TOK, 1024], BF16) for c in range(2)]

    arena = es.enter_context(nc.sbuf_tensor("arena", [P, 212480 // 4], F32))
    base = nc.lookup_mloc(arena).addr
    K = 1024
    R_A, R_B, R_C, R_D, R_E, R_S = 0, 32 * K, 96 * K, 128 * K, 160 * K, 192 * K

    def sb(name, shape, dt, off):
        return nc.alloc_sbuf_tensor_at(name, list(shape), dt, offset=base + off)

    xT = sb("xT", [P, KT, TOK], BF16, R_A)
    xres = sb("xres", [P, NT, D], F32, R_B)
    wbB = [sb(f"wbB{i}", [P, KT, 512], BF16, R_B + i * 16 * K) for i in range(4)]
    mT = sb("mT", [P, KT, TOK], BF16, R_C)
    zT = sb("zT", [P, KT, TOK], BF16, R_D)
    gb = sb("gb", [P, D], F32, R_D)
    bb = sb("bb", [P, D], F32, R_D + 8 * K)
    vtmp = [sb(f"vtmp{i}", [P, D], F32, R_D + 16 * K + i * 8 * K) for i in range(2)]
    tmpb = sb("tmpb", [P, D], BF16, R_D + 16 * K)
    sgt = [sb(f"sgt{i}", [P, 512], F32, R_D + 20 * K + i * 2 * K) for i in range(2)]
    wpb = sb("wpb", [P, 2, D], BF16, R_D + 24 * K)
    vn = sb("vn", [P, NT, D], BF16, R_E)
    wbE = [sb(f"wbE{i}", [P, KT, 512], BF16, R_E + i * 16 * K) for i in range(2)]
    wsb = sb("wsb", [P, 16, P], BF16, R_A)
    bf2 = sb("bf2", [2, D], F32, R_A + 4 * K)
    blo = sb("blo", [2, D], F32, R_A + 12 * K)
    btA = sb("btA", [2, D], F32, R_A + 20 * K)
    bhi = sb("bhi", [2, D], BF16, R_A + 28 * K)
    brow = sb("brow", [2, D], BF16, R_A + 4 * K)
    kst = [sb(f"kst{i}", [P, TOK], BF16, R_B + i * 2 * K) for i in range(2)]
    vst = [sb(f"vst{i}", [P, 512], BF16, R_B + 4 * K + i * K) for i in range(4)]
    kTh = [sb(f"kTh{i}", [P, 2, TOK], BF16, R_A + i * 4 * K) for i in range(2)]
    vh = [sb(f"vh{i}", [P, 2, NT, P], BF16, R_A + 8 * K + i * 4 * K) for i in range(2)]
    NPT = 6
    PT = [sb(f"PT{i}", [P, 256], BF16, R_A + 25 * K + i * 512) for i in range(NPT)]
    rden = [sb(f"rden{i}", [P, 256], F32, R_A + 18 * K + i * K) for i in range(2)]
    otmp = [sb(f"otmp{i}", [P, 256], F32, R_A + 20 * K + i * K) for i in range(2)]
    km32 = sb("km32", [P, 8], F32, R_A + 22 * K)
    kmb = sb("kmb", [P, 8], BF16, R_A + 22 * K + 64)
    gm = sb("gm", [P, 4, 8], F32, R_A + 22 * K + 128)
    top8 = sb("top8", [P, 4, 8], F32, R_A + 22 * K + 256)
    selb = sb("selb", [P, 4, 8], F32, R_A + 22 * K + 384)
    selbb = sb("selbb", [P, 4, 8], BF16, R_A + 22 * K + 512)
    selT = [sb(f"selT{i}", [P, 512], BF16, R_A + 23 * K + i * K) for i in range(2)]
    o = R_S
    ident = sb("ident", [P, P], BF16, o); o += 256
    ones = sb("ones", [P, P], BF16, o); o += 256
    cmb = sb("cmb", [P, 2, P], BF16, o); o += 512
    m01 = sb("m01", [P, P], BF16, o); o += 256
    eoh = sb("eoh", [P, 8, P], BF16, o); o += 2048
    pTb = sb("pTb", [P, 2, TOK], BF16, o); o += 4096
    cstf = sb("cstf", [P, 64], F32, o); o += 256
    stats_l = [sb(f"stats{i}", [P, 4, 6], F32, o + i * 96) for i in range(2)]; o += 192
    mv_l = [sb(f"mv{i}", [P, 2], F32, o + i * 32) for i in range(2)]; o += 64
    sd_l = [sb(f"sd{i}", [P, 1], F32, o + i * 32) for i in range(2)]; o += 64
    rstd_l = [sb(f"rstd{i}", [P, 1], F32, o + i * 32) for i in range(2)]; o += 64
    nb_l = [sb(f"nb{i}", [P, 1], F32, o + i * 32) for i in range(2)]; o += 64
    assert o <= 212480 - 64

    ps = [es.enter_context(nc.psum_tensor(f"ps{i}", [P, 512], F32)) for i in range(6)]
    pstl = [es.enter_context(nc.psum_tensor(f"pst{i}", [P, 1024], BF16)) for i in range(2)]

    nsem = [0]

    def newsem(name):
        nsem[0] += 1
        return es.enter_context(nc.semaphore(f"{name}_{nsem[0]}"))

    PE = Queue("pe", newsem("pe"))
    ACT = Queue("act", newsem("act"))
    DVE = Queue("dve", newsem("dve"))
    POOL = Queue("pool", newsem("pool"))
    SP = Queue("sp", newsem("sp"))

    def dsem(name):
        return DSem(newsem(name))

    bank_free = [None] * 4
    bank_i = [0]

    def next_bank():
        b = bank_i[0] % 4
        bank_i[0] += 1
        return b

    def wsrc(w_ap, col0, ncol=512, kt=KT):
        return w_ap.rearrange("(kt p) n -> p kt n", p=P)[:, :, col0:col0 + ncol]

    class WStream:
        def __init__(self, bufs, name):
            self.bufs = bufs
            self.free = [None] * len(bufs)
            self.sems = [dsem(f"{name}{i}") for i in range(len(bufs))]
            self.i = 0

        def load(self, src, extra_wait=None):
            b = self.i % len(self.bufs)
            self.i += 1
            POOL.wait(self.free[b], extra_wait)
            buf = self.bufs[b]
            tok = POOL.dma(self.sems[b], lambda e, buf=buf, src=src: e.dma_start(out=buf[:, :, :], in_=src))
            return b, tok

    def mm_group(out_ap, pairs, first_waits=()):
        n = len(pairs)
        PE.wait(*first_waits)
        tok = None
        for k, (l, r) in enumerate(pairs):
            tok = PE.do(lambda e, l=l, r=r, k=k: e.matmul(out_ap, l, r, start=(k == 0), stop=(k == n - 1)),
                        sig=(k == n - 1))
        return tok

    ln_cnt = [0]
    ln_war = [None, None]

    def ln_tile(src, free_waits):
        k = ln_cnt[0] % 2
        ln_cnt[0] += 1
        stats, mv, sd, rstd = stats_l[k], mv_l[k], sd_l[k], rstd_l[k]
        DVE.wait(*free_waits)
        DVE.wait(ln_war[k])
        for c in range(4):
            t = DVE.do(lambda e, c=c: e.bn_stats(out=stats[:, c, :], in_=src[:, c * 512:(c + 1) * 512]))
        DVE.wait(t)
        t = DVE.do(lambda e: e.bn_aggr(out=mv[:, :], in_=stats[:, :, :].rearrange("p a b -> p (a b)")))
        ACT.wait(t)
        t = ACT.do(lambda e: e.activation(out=sd[:, :], in_=mv[:, 1:2], func=AF.Sqrt, bias=EPS, scale=1.0))
        DVE.wait(t)
        t = DVE.do(lambda e: e.reciprocal(out=rstd[:, :], in_=sd[:, :]))
        DVE.wait(t)
        return mv, rstd, k

    def transposes_to(dstT, i, src_bf, src_tok, war_tok, evq=None):
        toks = []
        for g4 in range(4):
            half = g4 % 2
            PE.wait(src_tok, tr_free[half])
            for j in range(4):
                kt = g4 * 4 + j
                t = PE.do(lambda e, kt=kt, j=j, half=half: e.transpose(
                    out=pstl[half][:, j * 128:(j + 1) * 128],
                    in_=src_bf[:, kt * 128:(kt + 1) * 128], identity=ident[:, :]), sig=(j == 3))
            if evq is ACT:
                ACT.wait(t, war_tok)
                t2 = ACT.do(lambda e, g4=g4, half=half: e.activation(
                    out=dstT[:, g4 * 4:(g4 + 1) * 4, i * 128:(i + 1) * 128],
                    in_=pstl[half][:, 0:512].rearrange("p (a b) -> p a b", a=4), func=AF.Copy))
            else:
                DVE.wait(t, war_tok)
                t2 = DVE.do(lambda e, g4=g4, half=half: e.tensor_copy(
                    out=dstT[:, g4 * 4:(g4 + 1) * 4, i * 128:(i + 1) * 128],
                    in_=pstl[half][:, 0:512].rearrange("p (a b) -> p a b", a=4)))
            tr_free[half] = t2
            toks.append(t2)
        return toks

    tr_free = [None, None]

    class _Done(Exception):
        pass

    fin = {}

    def dump(srcs, tok):
        toks = []
        s_d = dsem("dbgdump")
        for i, src in enumerate(srcs):
            ACT.wait(tok)
            t = ACT.do(lambda e, i=i, src=src: e.activation(out=xres[:, i, :], in_=src, func=AF.Copy))
            SP.wait(t)
            toks.append(SP.dma(s_d, lambda e, i=i: e.dma_start(out=out[i * 128:(i + 1) * 128, :], in_=xres[:, i, :])))
        fin["toks"] = toks
        raise _Done()

    def phases():
        s_cst = dsem("cst")
        t_cst = [
            POOL.dma(s_cst, lambda e: e.dma_start(out=m01[:, :], in_=cst_in[:, C_M01:C_M01 + 128])),
            POOL.dma(s_cst, lambda e: e.dma_start(out=cmb[:, 0, :], in_=cst_in[:, C_CM0:C_CM0 + 128])),
            POOL.dma(s_cst, lambda e: e.dma_start(out=cmb[:, 1, :], in_=cst_in[:, C_CM1:C_CM1 + 128])),
            POOL.dma(s_cst, lambda e: e.dma_start(out=ident[:, :], in_=cst_in[:, C_ID:C_ID + 128])),
            POOL.dma(s_cst, lambda e: e.dma_start(out=eoh[:, :, :].rearrange("p a b -> p (a b)"), in_=cste_in[:, :])),
            POOL.dma(s_cst, lambda e: e.dma_start(out=cstf[:, :], in_=cst_in[:, C_FB:C_FB + 64])),
        ][-1]
        t_ones = DVE.do(lambda e: e.memset(ones[:, :], 1.0))

        s_xT = [dsem(f"xT{i}") for i in range(4)]
        xTv = xT_in.rearrange("(kt p) t -> p kt t", p=P)
        def xT_load(q):
            return POOL.dma(s_xT[q], lambda e, q=q: e.dma_start(out=xT[:, :, q * 256:(q + 1) * 256],
                                                               in_=xTv[:, :, q * 256:(q + 1) * 256]))
        t_xT = [xT_load(0), None, None, None]
        wsB = WStream(wbB, "wB")
        wsE = WStream(wbE, "wE")
        if dbg == "P0":
            for q in range(1, 4):
                t_xT[q] = xT_load(q)
            dump([xT[:, 2 * i:2 * i + 2, :].rearrange("p a b -> p (a b)") for i in range(NT)], t_xT + [t_cst])

        wv_tok = []
        for c in range(4):
            b, t = wsB.load(wsrc(w_in_a, D + c * 512))
            wv_tok.append(t)
        for q in range(1, 4):
            t_xT[q] = xT_load(q)
        s_par = dsem("par")
        t_par = [SP.dma(s_par, lambda e: e.dma_start(out=gb[:, :], in_=sgu_g.to_broadcast([P, D]))),
                 SP.dma(s_par, lambda e: e.dma_start(out=bb[:, :], in_=sgu_b.to_broadcast([P, D])))][-1]

        vtmp_free = [None, None]
        last_v_mm = [None] * 4
        pre_act = {0: [], 1: []}

        def v_unit(i, c):
            vt = vtmp[i % 2]
            b = next_bank()
            pairs = [(xT[:, kt, i * 128:(i + 1) * 128], wbB[c][:, kt, :]) for kt in range(KT)]
            tmm = mm_group(ps[b][:, :], pairs, [bank_free[b], t_xT[i // 2], wv_tok[c]])
            last_v_mm[c] = tmm
            ACT.wait(tmm, vtmp_free[i % 2])
            ta = ACT.do(lambda e, b=b, c=c, vt=vt: e.activation(out=vt[:, c * 512:(c + 1) * 512], in_=ps[b][:, :],
                                                              func=AF.Gelu_apprx_tanh))
            bank_free[b] = ta
            return ta

        for c in range(4):
            for i in (0, 1):
                pre_act[i].append(v_unit(i, c))
        for i in range(NT):
            vt = vtmp[i % 2]
            if i < 2:
                act_toks = pre_act[i]
            else:
                act_toks = [v_unit(i, c) for c in range(4)]
            mv, rstd, lk = ln_tile(vt, act_toks)
            DVE.wait(t_par)
            t = DVE.do(lambda e, vt=vt, mv=mv: e.scalar_tensor_tensor(out=vt[:, :], in0=vt[:, :], scalar=mv[:, 0:1],
                                                                    in1=gb[:, :], op0=ALU.subtract, op1=ALU.mult))
            DVE.wait(t)
            t = DVE.do(lambda e, vt=vt, i=i, rstd=rstd: e.scalar_tensor_tensor(
                out=vn[:, i, :], in0=vt[:, :], scalar=rstd[:, 0:1], in1=bb[:, :], op0=ALU.mult, op1=ALU.add))
            ln_war[lk] = t
            vtmp_free[i % 2] = t
        t_vn_done = t
        for c in range(4):
            wsB.free[c] = last_v_mm[c]
        if dbg == "V":
            dump([vn[:, i, :] for i in range(NT)], [t_vn_done, last_v_mm])

        chunks = [("u", c) for c in range(4)] + [("z", c) for c in range(4)]
        loads = {}

        def uz_load(n):
            kind, c = chunks[n]
            col = c * 512 if kind == "u" else 2 * D + c * 512
            loads[n] = wsB.load(wsrc(w_in_a, col))

        LA = 3
        for n in range(min(LA, len(chunks))):
            uz_load(n)
        t_z_war = t_vn_done
        last_act_uz = None
        for n, (kind, c) in enumerate(chunks):
            if n + LA < len(chunks):
                uz_load(n + LA)
            wb_i, wtok = loads[n]
            dst = mT if kind == "u" else zT
            fn = AF.Gelu_apprx_tanh if kind == "u" else AF.Silu
            for ct in range(4):
                ctile = c * 4 + ct
                for half in range(2):
                    b = next_bank()
                    pairs = [(wbB[wb_i][:, kt, ct * 128:(ct + 1) * 128], xT[:, kt, half * 512:(half + 1) * 512])
                             for kt in range(KT)]
                    tmm = mm_group(ps[b][:, :], pairs, [bank_free[b], wtok, t_xT[2 * half], t_xT[2 * half + 1]])
                    ACT.wait(tmm, t_z_war if kind == "z" else None)
                    ta = ACT.do(lambda e, b=b, dst=dst, ctile=ctile, half=half, fn=fn: e.activation(
                        out=dst[:, ctile, half * 512:(half + 1) * 512], in_=ps[b][:, :], func=fn))
                    bank_free[b] = ta
                    last_act_uz = ta
                if kind == "z":
                    DVE.wait(ta)
                    last_act_uz = DVE.do(lambda e, ctile=ctile: e.tensor_tensor(
                        out=mT[:, ctile, :], in0=mT[:, ctile, :], in1=zT[:, ctile, :], op=ALU.mult))
            wsB.free[wb_i] = tmm
        t_uz_mm_done = tmm
        t_uz_done = last_act_uz
        if dbg == "UZ":
            dump([mT[:, 2 * i:2 * i + 2, :].rearrange("p a b -> p (a b)") for i in range(NT)],
                 [t_uz_done, t_uz_mm_done])

        s_s = dsem("sgu")
        POOL.wait(t_uz_mm_done)
        t_ws = POOL.dma(s_s, lambda e: e.dma_start(out=wsb[:, :, :], in_=wsT_in[:, :, :]))
        s_s2 = dsem("sgub")
        SP.wait(t_uz_mm_done)
        SP.dma(s_s2, lambda e: e.dma_start(out=bf2[0:1, :], in_=bs_in[:, :]))
        t_bs = SP.dma(s_s2, lambda e: e.dma_start(out=bf2[1:2, :], in_=bs_in[:, :]))
        DVE.wait(t_bs, t_cst)
        t = DVE.do(lambda e: e.tensor_copy(out=bhi[:, :], in_=bf2[:, :]))
        DVE.wait(t)
        t = DVE.do(lambda e: e.tensor_tensor(out=blo[:, :], in0=bf2[:, :], in1=bhi[:, :], op=ALU.subtract))
        t = DVE.do(lambda e: e.tensor_scalar(out=btA[:, :], in0=bhi[:, :], scalar1=ident[0:2, 0:1], scalar2=None,
                                             op0=ALU.mult))
        DVE.wait(t)
        t_brow = DVE.do(lambda e: e.scalar_tensor_tensor(out=brow[:, :], in0=blo[:, :], scalar=ident[0:2, 1:2],
                                                         in1=btA[:, :], op0=ALU.mult, op1=ALU.add))
        DVE.wait(t_ws)
        for g in range(16):
            t = DVE.do(lambda e, g=g: e.tensor_tensor(out=wsb[:, g, :], in0=wsb[:, g, :], in1=m01[:, :], op=ALU.mult),
                       sig=(g == 15))
        t_wsb = t
        t_s_last = None
        for g in range(16):
            for nh in range(2):
                b = next_bank()
                PE.wait(bank_free[b], t_wsb, t_vn_done, t_brow, t_ones)
                for j in range(4):
                    n = nh * 4 + j
                    PE.do(lambda e, b=b, j=j, n=n, g=g: e.matmul(
                        ps[b][:, j * 128:(j + 1) * 128], vn[:, n, g * 128:(g + 1) * 128], wsb[:, g, :],
                        start=True, stop=False, skip_group_check=True), sig=False)
                    tmm = PE.do(lambda e, b=b, j=j, g=g: e.matmul(
                        ps[b][:, j * 128:(j + 1) * 128], ones[0:2, :], brow[0:2, g * 128:(g + 1) * 128],
                        start=False, stop=True, skip_group_check=True), sig=(j == 3))
                DVE.wait(tmm, last_act_uz)
                t = DVE.do(lambda e, b=b, g=g, nh=nh: e.tensor_tensor(
                    out=mT[:, g, nh * 512:(nh + 1) * 512], in0=ps[b][:, :], in1=mT[:, g, nh * 512:(nh + 1) * 512],
                    op=ALU.mult))
                bank_free[b] = t
                t_s_last = t
        t_s_mm_done = tmm
        if dbg == "S":
            dump([mT[:, 2 * i:2 * i + 2, :].rearrange("p a b -> p (a b)") for i in range(NT)],
                 [t_s_last, t_s_mm_done])

        def x_out(l, xsrc, t_m_ready, t_rd_free, t_re_free, t_rb_free, t_ra_free, final):
            s_x = [dsem(f"xr{l}_{i}") for i in range(NT)]
            SP.wait(t_rb_free)
            t_x = [SP.dma(s_x[i], lambda e, i=i: e.dma_start(out=xres[:, i, :], in_=xsrc[i * 128:(i + 1) * 128, :]))
                   for i in range(NT)]
            s_p = dsem(f"par{l}")
            SP.wait(t_rd_free)
            SP.dma(s_p, lambda e: e.dma_start(out=gb[:, :], in_=ln_g[l:l + 1, :].to_broadcast([P, D])))
            t_par = SP.dma(s_p, lambda e: e.dma_start(out=bb[:, :], in_=ln_b[l:l + 1, :].to_broadcast([P, D])))
            wl = [wsrc(w_out[l], c * 512) for c in range(4)] + [wsrc(w_gate[l], c * 512) for c in range(4)]
            ld = {}
            ld[0] = wsE.load(wl[0], t_re_free)
            ld[1] = wsE.load(wl[1], t_re_free)
            s_pp = dsem(f"pp{l}")
            POOL.wait(t_rd_free)
            POOL.dma(s_pp, lambda e: e.dma_start(out=pTb[:, :, :], in_=pT_in[l].rearrange("(k p) t -> p k t", p=P)))
            t_pp = POOL.dma(s_pp, lambda e: e.dma_start(out=wpb[:, :, :],
                                                       in_=w_proj[l].rearrange("(k p) n -> p k n", p=P)))
            r_tok = [[None] * 4 for _ in range(NT)]
            xT_tok = [None] * NT
            st = {"tmpb_free": None}
            lnv = [None] * NT
            t1b = [None] * NT
            t_nb = [None] * NT

            def s1a(i):
                lnv[i] = ln_tile(xres[:, i, :], r_tok[i])
                mv, rstd, lk = lnv[i]
                nb = nb_l[lk]
                t_nb[i] = DVE.do(lambda e, mv=mv, rstd=rstd, nb=nb: e.tensor_scalar(
                    out=nb[:, :], in0=mv[:, 0:1], scalar1=rstd[:, 0:1], scalar2=-1.0, op0=ALU.mult, op1=ALU.mult))

            def s1b(i):
                xi = xres[:, i, :]
                mv, rstd, lk = lnv[i]
                nb = nb_l[lk]
                ACT.wait(t_nb[i])
                t = ACT.do(lambda e, xi=xi, rstd=rstd, nb=nb: e.activation(out=xi, in_=xi, func=AF.Identity,
                                                                         bias=nb[:, 0:1], scale=rstd[:, 0:1]))
                ln_war[lk] = t
                DVE.wait(t, t_par)
                t1b[i] = DVE.do(lambda e, xi=xi: e.tensor_tensor(out=xi, in0=xi, in1=gb[:, :], op=ALU.mult))

            def s2(i):
                xi = xres[:, i, :]
                POOL.wait(t1b[i], t_par)
                t = POOL.do(lambda e, xi=xi: e.tensor_tensor(out=xi, in0=xi, in1=bb[:, :], op=ALU.add))
                ACT.wait(t, st["tmpb_free"])
                tc = ACT.do(lambda e, xi=xi: e.activation(out=tmpb[:, :], in_=xi, func=AF.Copy))
                toks = transposes_to(xT, i, tmpb, tc, t_ra_free, evq=ACT)
                st["tmpb_free"] = toks[-1]
                xT_tok[i] = toks

            ln_step = [0]

            def ln_advance():
                step = ln_step[0]
                ln_step[0] += 1
                if step < NT:
                    s1a(step)
                if 1 <= step <= NT:
                    s1b(step - 1)
                if step >= 2:
                    s2(step - 2)

            for c in range(4):
                wb_i, wtok = ld[c]
                for i in range(NT):
                    b = next_bank()
                    pairs = [(mT[:, kt, i * 128:(i + 1) * 128], wbE[wb_i][:, kt, :]) for kt in range(KT)]
                    tmm = mm_group(ps[b][:, :], pairs, [bank_free[b], wtok, t_m_ready])
                    DVE.wait(tmm, t_x[i])
                    t = DVE.do(lambda e, b=b, i=i, c=c: e.scalar_tensor_tensor(
                        out=xres[:, i, c * 512:(c + 1) * 512], in0=xres[:, i, c * 512:(c + 1) * 512], scalar=ALPHA,
                        in1=ps[b][:, :], op0=ALU.mult, op1=ALU.add))
                    bank_free[b] = t
                    r_tok[i][c] = t
                wsE.free[wb_i] = tmm
                if c + 2 < 8:
                    ld[c + 2] = wsE.load(wl[c + 2])
            if dbg == "O1":
                s_d = dsem("dbgo1")
                SP.wait([r_tok[i] for i in range(NT)])
                fin["toks"] = [SP.dma(s_d, lambda e, i=i: e.dma_start(out=out[i * 128:(i + 1) * 128, :],
                                                                    in_=xres[:, i, :])) for i in range(NT)]
                raise _Done()
            while ln_step[0] < NT + 2:
                ln_advance()
            tmpb_free = st["tmpb_free"]
            if dbg == "O2":
                dump([xT[:, 2 * i:2 * i + 2, :].rearrange("p a b -> p (a b)") for i in range(NT)],
                     [xT_tok[i] for i in range(NT)])
            fin_tok = [None] * NT
            sg_free = [None, None]
            kk = 0
            out_sems = dsem(f"out{l}")
            deferred = None
            res = {}

            def finish_tile(i):
                if final:
                    SP.wait(fin_tok[i])
                    res.setdefault("out", []).append(
                        SP.dma(out_sems, lambda e, i=i: e.dma_start(out=out[i * 128:(i + 1) * 128, :], in_=xres[:, i, :])))
                else:
                    SP.wait(fin_tok[i])
                    res.setdefault("out", []).append(
                        SP.dma(out_sems, lambda e, i=i: e.dma_start(out=x1sp[i * 128:(i + 1) * 128, :], in_=xres[:, i, :])))
                    ACT.wait(fin_tok[i], res.get("tmpb_free"))
                    tc = ACT.do(lambda e, i=i: e.activation(out=tmpb[:, :], in_=xres[:, i, :], func=AF.Copy))
                    toks = transposes_to(xT, i, tmpb, tc, res["gate_mm_last"][i], evq=ACT)
                    res["tmpb_free"] = toks[-1]
                    res.setdefault("x1T", [None] * NT)[i] = toks

            res["gate_mm_last"] = [None] * NT
            res["tmpb_free"] = tmpb_free
            for c in range(4):
                wb_i, wtok = ld[4 + c]
                for i in range(NT):
                    bg = next_bank()
                    pairs = [(xT[:, kt, i * 128:(i + 1) * 128], wbE[wb_i][:, kt, :]) for kt in range(KT)]
                    tg = mm_group(ps[bg][:, :], pairs, [bank_free[bg], wtok] + xT_tok[i])
                    if c == 3:
                        res["gate_mm_last"][i] = tg
                    bp = next_bank()
                    pairs = [(pTb[:, k2, i * 128:(i + 1) * 128], wpb[:, k2, c * 512:(c + 1) * 512]) for k2 in range(2)]
                    tp = mm_group(ps[bp][:, :], pairs, [bank_free[bp], t_pp])
                    sg = sgt[kk % 2]
                    ACT.wait(tg, sg_free[kk % 2])
                    ta = ACT.do(lambda e, bg=bg, sg=sg: e.activation(out=sg[:, :], in_=ps[bg][:, :], func=AF.Sigmoid))
                    bank_free[bg] = ta
                    DVE.wait(ta, tp)
                    t = DVE.do(lambda e, bp=bp, sg=sg: e.tensor_tensor(out=sg[:, :], in0=sg[:, :], in1=ps[bp][:, :],
                                                                     op=ALU.mult))
                    bank_free[bp] = t
                    DVE.wait(t)
                    t = DVE.do(lambda e, i=i, c=c, sg=sg: e.tensor_tensor(
                        out=xres[:, i, c * 512:(c + 1) * 512], in0=xres[:, i, c * 512:(c + 1) * 512], in1=sg[:, :],
                        op=ALU.add))
                    sg_free[kk % 2] = t
                    kk += 1
                    if c == 3:
                        fin_tok[i] = t
                        if deferred is not None:
                            finish_tile(deferred)
                        deferred = i
                wsE.free[wb_i] = tg
                if 4 + c + 2 < 8:
                    ld[4 + c + 2] = wsE.load(wl[4 + c + 2])
            finish_tile(deferred)
            res["last_dve"] = t
            res["last_gate_mm"] = tg
            return res

        r0 = x_out(0, x_in, [t_s_last], [t_s_last], [t_s_mm_done], [t_uz_mm_done], [t_s_last],
                   final=(dbg in ("L0", "O1", "O2")))

        final_out_toks = r0["out"]
        if dbg not in ("L0", "O1", "O2"):
            x1T_tok = r0["x1T"]
            t_rb_free = r0["out"]
            kload = {}
            kv_chunks = [("k", c) for c in range(4)] + [("v", c) for c in range(4)] + \
                        [("q", c) for c in range(4)] + [("z", c) for c in range(4)]

            wsL = WStream([wbE[0], wbE[1], wbB[1], wbB[2], wbB[3]], "wL")
            wsL.free = [wsE.free[0], wsE.free[1], t_rb_free, t_rb_free, t_rb_free]
            LAK = 4

            def kv_load(n):
                kind, c = kv_chunks[n]
                col = {"q": 0, "k": D, "v": 2 * D, "z": 3 * D}[kind] + c * 512
                kload[n] = wsL.load(wsrc(w_in_b, col))

            for n in range(LAK):
                kv_load(n)
            s_kst = [dsem(f"kst{i}") for i in range(2)]
            s_vst = [dsem(f"vst{i}") for i in range(4)]
            kst_free = [None, None]
            vst_free = [None] * 4
            s_cc = [dsem(f"cc{i}") for i in range(4)]
            cc_tok = {}
            kstores = [[], []]
            vstores = [[], []]
            kslot = 0
            vslot = 0
            t_qz_war = r0["last_dve"]
            for n, (kind, c) in enumerate(kv_chunks):
                wb_i, wtok = kload[n]
                if kind in ("k", "q", "z"):
                    for hh in range(4):
                        h = c * 4 + hh
                        for half in range(2):
                            b = next_bank()
                            pairs = [(wsL.bufs[wb_i][:, kt, hh * 128:(hh + 1) * 128],
                                      xT[:, kt, half * 512:(half + 1) * 512]) for kt in range(KT)]
                            fw = [bank_free[b], wtok]
                            for i in range(half * 4, half * 4 + 4):
                                fw += x1T_tok[i]
                            tmm = mm_group(ps[b][:, :], pairs, fw)
                            if kind == "k":
                                sl = kslot % 2
                                ACT.wait(tmm, kst_free[sl], t_rb_free if kslot < 2 else None)
                                ta = ACT.do(lambda e, b=b, sl=sl, half=half: e.activation(
                                    out=kst[sl][:, half * 512:(half + 1) * 512], in_=ps[b][:, :], func=AF.Copy))
                            elif kind == "q":
                                ACT.wait(tmm, t_qz_war)
                                ta = ACT.do(lambda e, b=b, h=h, half=half: e.activation(
                                    out=zT[:, h, half * 512:(half + 1) * 512], in_=ps[b][:, :], func=AF.Copy,
                                    scale=QSCALE))
                            else:
                                ACT.wait(tmm, t_qz_war, r0["last_gate_mm"])
                                ta = ACT.do(lambda e, b=b, h=h, half=half: e.activation(
                                    out=mT[:, h, half * 512:(half + 1) * 512], in_=ps[b][:, :], func=AF.Silu))
                            bank_free[b] = ta
                        if kind == "k":
                            sl = kslot % 2
                            SP.wait(ta)
                            ts = SP.dma(s_kst[sl], lambda e, sl=sl, h=h: e.dma_start(
                                out=mineK[h // 8][(h % 8) * 128:(h % 8 + 1) * 128, :], in_=kst[sl][:, :]))
                            kst_free[sl] = ts
                            kstores[h // 8].append(ts)
                            kslot += 1
                    if kind == "k" and c % 2 == 1:
                        cidx = c // 2
                        POOL.wait(kstores[cidx])
                        cc_tok[("k", cidx)] = POOL.cc(s_cc[cidx], lambda e, cidx=cidx: e.collective_compute(
                            "AllGather", ALU.bypass, replica_groups=PAIRS,
                            ins=[mineK[cidx].ap().opt()], outs=[allK[cidx].ap().opt()]))
                else:
                    for i in range(NT):
                        b = next_bank()
                        pairs = [(xT[:, kt, i * 128:(i + 1) * 128], wsL.bufs[wb_i][:, kt, :]) for kt in range(KT)]
                        tmm = mm_group(ps[b][:, :], pairs, [bank_free[b], wtok] + x1T_tok[i])
                        sl = vslot % 4
                        ACT.wait(tmm, vst_free[sl])
                        ta = ACT.do(lambda e, b=b, sl=sl: e.activation(out=vst[sl][:, :], in_=ps[b][:, :], func=AF.Copy))
                        bank_free[b] = ta
                        SP.wait(ta)
                        ts = SP.dma(s_vst[sl], lambda e, sl=sl, i=i, c=c: e.dma_start(
                            out=mineV[c // 2][i * 128:(i + 1) * 128, (c % 2) * 512:(c % 2 + 1) * 512], in_=vst[sl][:, :]))
                        vst_free[sl] = ts
                        vstores[c // 2].append(ts)
                        vslot += 1
                    if c % 2 == 1:
                        cidx = c // 2
                        POOL.wait(vstores[cidx])
                        cc_tok[("v", cidx)] = POOL.cc(s_cc[2 + cidx], lambda e, cidx=cidx: e.collective_compute(
                            "AllGather", ALU.bypass, replica_groups=PAIRS,
                            ins=[mineV[cidx].ap().opt()], outs=[allV[cidx].ap().opt()]))
                wsL.free[wb_i] = tmm
                if n + LAK < len(kv_chunks):
                    kv_load(n + LAK)
            wsE.free = [wsL.free[0], wsL.free[1]]
            t_qz_mm_done = tmm
            t_qz_done = ta
            if dbg == "Q":
                SP.wait([cc_tok[k] for k in cc_tok])
                dump([zT[:, 2 * i:2 * i + 2, :].rearrange("p a b -> p (a b)") for i in range(NT)],
                     [t_qz_done, t_qz_mm_done] + r0["out"])

            s_kv = [dsem(f"kvh{i}") for i in range(2)]
            kv_free = [None, None]
            kv_tok = {}

            def att_load(h):
                sl = h % 2
                cidx = h // 8
                hh = h % 8
                SP.wait(kv_free[sl], cc_tok[("k", cidx)], cc_tok[("v", cidx)], t_qz_mm_done)
                ksrc = allK[cidx].ap().rearrange("(r hh d) t -> d r hh t", r=2, hh=8)[:, :, hh, :]
                SP.dma(s_kv[sl], lambda e, sl=sl, ksrc=ksrc: e.dma_start(out=kTh[sl][:, :, :], in_=ksrc))
                t = None
                for r in range(2):
                    vsrc = allV[cidx].ap().rearrange("(r i p) (hh d) -> p r i hh d", r=2, i=NT, hh=8)[:, r, :, hh, :]
                    t = SP.dma(s_kv[sl], lambda e, sl=sl, vsrc=vsrc, r=r: e.dma_start(out=vh[sl][:, r, :, :], in_=vsrc))
                kv_tok[h] = t

            att_load(0)
            DVE.wait(t_qz_mm_done)
            DVE.do(lambda e: e.memset(selT[0][:, :], 0.0))
            DVE.do(lambda e: e.memset(selT[1][:, :], 0.0))
            S = [ps[2][:, 0:256], ps[3][:, 0:256], ps[4][:, 0:256], ps[5][:, 0:256]]
            OA = [ps[0][:, 0:256], ps[1][:, 0:256]]
            DA = [ps[0][:, 256:512], ps[1][:, 256:512]]
            psg = pstl[1].bitcast(F32)
            s_free = [[bank_free[2], bank_free[3]]] * 4
            pt_free = [None] * NPT
            acc_free = [bank_free[0], bank_free[1]]
            acc_free = [[bank_free[0], bank_free[1], bank_free[2], bank_free[3]]] * 2
            selT_free = [None, None]
            gidx = 0
            aidx = 0
            t_att_dve = None
            t_kmb_h = {}
            t_selbb_h = {}
            t_sel_h = {}

            def prepA_dve(hp):
                kTp = kTh[hp % 2]
                DVE.wait(kv_tok[hp])
                t = DVE.do(lambda e, kTp=kTp: e.tensor_reduce(
                    out=km32[:, :], in_=kTp[:, :, :].rearrange("p r (j t) -> p j r t", j=NT), axis=AX.XY, op=ALU.add))
                DVE.wait(t)
                t_kmb_h[hp] = DVE.do(lambda e: e.tensor_scalar(out=kmb[:, :], in0=km32[:, :], scalar1=1.0 / 256.0,
                                                               scalar2=None, op0=ALU.mult))

            def prepA_pe(hp):
                PE.wait(t_kmb_h[hp], t_qz_done, tr_free[1])
                for ii in range(4):
                    i = 4 + ii
                    tg = PE.do(lambda e, ii=ii, i=i, hp=hp: e.matmul(psg[:, ii * 8:(ii + 1) * 8],
                                                                    zT[:, hp, i * 128:(i + 1) * 128], kmb[:, :],
                                                                    start=True, stop=True), sig=(ii == 3))
                DVE.wait(tg, t_cst)
                t = DVE.do(lambda e: e.tensor_tensor(out=gm[:, :, :].rearrange("p a b -> p (a b)"), in0=psg[:, 0:32],
                                                     in1=cstf[:, 0:32], op=ALU.add))
                DVE.wait(t)
                for ii in range(4):
                    t = DVE.do(lambda e, ii=ii: e.max(out=top8[:, ii, :], in_=gm[:, ii, :]))
                DVE.wait(t)
                for ii in range(4):
                    t = DVE.do(lambda e, ii=ii: e.tensor_scalar(out=selb[:, ii, :], in0=gm[:, ii, :],
                                                               scalar1=top8[:, ii, 2:3], scalar2=1.0e12,
                                                               op0=ALU.subtract, op1=ALU.mult))
                DVE.wait(t)
                t = DVE.do(lambda e: e.tensor_scalar(out=selb[:, :, :].rearrange("p a b -> p (a b)"),
                                                     in0=selb[:, :, :].rearrange("p a b -> p (a b)"),
                                                     scalar1=0.0, scalar2=None, op0=ALU.min))
                DVE.wait(t)
                t_selbb_h[hp] = DVE.do(lambda e: e.tensor_tensor(out=selbb[:, :, :].rearrange("p a b -> p (a b)"),
                                                                 in0=selb[:, :, :].rearrange("p a b -> p (a b)"),
                                                                 in1=cstf[:, 32:64], op=ALU.mult))

            def prepB(hp):
                sTp = selT[hp % 2]
                PE.wait(t_selbb_h[hp], tr_free[0])
                for ii in range(4):
                    tt = PE.do(lambda e, ii=ii: e.transpose(out=pstl[0][0:8, ii * 128:(ii + 1) * 128],
                                                           in_=selbb[:, ii, :], identity=ident[:, :]), sig=(ii == 3))
                DVE.wait(tt, selT_free[hp % 2])
                t_sel_h[hp] = DVE.do(lambda e, sTp=sTp: e.tensor_copy(out=sTp[0:8, :], in_=pstl[0][0:8, 0:512]))
                tr_free[0] = t_sel_h[hp]

            for h in range(16):
                sl = h % 2
                if h + 1 < 16:
                    att_load(h + 1)
                kT_ = kTh[sl]
                v_ = vh[sl]
                sT = selT[sl]
                if h == 0:
                    prepA_dve(0)
                    prepA_pe(0)
                    prepB(0)
                t_sel = t_sel_h[h]
                for m in range(4):
                    if h + 1 < 16:
                        if m == 1:
                            prepA_dve(h + 1)
                        elif m == 2:
                            prepA_pe(h + 1)
                        elif m == 3:
                            prepB(h + 1)
                    a = 0 if os.environ.get('KDBG_A0') else aidx % 2
                    aidx += 1
                    tiles = []
                    for j in range(2 * m + 2):
                        for t_ in range(2):
                            c0 = 0 if j <= 2 * m else 128
                            tiles.append((j, t_, c0))
                    ntile = len(tiles)
                    ex_tok = [None] * ntile
                    slots = [None] * ntile
                    q0 = m * 256

                    def emit_qk(n):
                        nonlocal gidx
                        j, t_, c0 = tiles[n]
                        s = gidx % 4
                        pslot = gidx % NPT
                        gidx += 1
                        slots[n] = pslot
                        ncol = 256 - c0
                        so = S[s][:, c0:256]
                        mms = [(kT_[:, t_, j * 128:(j + 1) * 128], zT[:, h, q0 + c0:q0 + 256], so)]
                        if m >= 2 and j < 2 * m + 1:
                            mms.append((eoh[:, j, :], sT[:, (m - 2) * 256 + c0:(m - 2) * 256 + 256], so))
                        if j == 2 * m:
                            mms.append((ident[:, :], cmb[:, t_, :], S[s][:, 0:128]))
                        if j == 2 * m + 1:
                            mms.append((ident[:, :], cmb[:, t_, :], S[s][:, 128:256]))
                        PE.wait(s_free[s], t_sel if m >= 2 else None, t_cst)
                        tk = None
                        for k, (l, r, oap) in enumerate(mms):
                            tk = PE.do(lambda e, l=l, r=r, oap=oap, k=k, last=(k == len(mms) - 1): e.matmul(
                                oap, l, r, start=(k == 0), stop=last, skip_group_check=True), sig=(k == len(mms) - 1))
                        ACT.wait(tk, pt_free[pslot])
                        ex_tok[n] = ACT.do(lambda e, s=s, pslot=pslot, c0=c0: e.activation(
                            out=PT[pslot][:, c0:256], in_=S[s][:, c0:256], func=AF.Exp))
                        s_free[s] = ex_tok[n]

                    def emit_pv(n):
                        j, t_, c0 = tiles[n]
                        s = slots[n]
                        PE.wait(ex_tok[n], acc_free[a] if n == 0 else None, t_ones)
                        PE.do(lambda e, s=s, c0=c0, j=j, t_=t_, n=n, a=a, v_=v_, ntile=ntile: e.matmul(
                            OA[a][:, c0:256], v_[:, t_, j, :], PT[s][:, c0:256], start=(n == 0), stop=(n == ntile - 1),
                            skip_group_check=True), sig=False)
                        tk = PE.do(lambda e, s=s, c0=c0, n=n, a=a, ntile=ntile: e.matmul(
                            DA[a][:, c0:256], ones[:, :], PT[s][:, c0:256], start=False, stop=(n == ntile - 1),
                            skip_group_check=True))
                        pt_free[s] = tk
                        return tk

                    LAQ = int(os.environ.get('KDBG_LAQ', '3'))
                    tk_last = None
                    for n in range(ntile):
                        emit_qk(n)
                        if n >= LAQ:
                            tk_last = emit_pv(n - LAQ)
                    for n in range(max(0, ntile - LAQ), ntile):
                        tk_last = emit_pv(n)
                    rd, ot = rden[a], otmp[a]
                    DVE.wait(tk_last)
                    t = DVE.do(lambda e, a=a, rd=rd: e.reciprocal(out=rd[:, :], in_=DA[a]))
                    DVE.wait(t)
                    t = DVE.do(lambda e, a=a, rd=rd, ot=ot: e.tensor_tensor(out=ot[:, :], in0=OA[a], in1=rd[:, :],
                                                                          op=ALU.mult))
                    acc_free[a] = t
                    DVE.wait(t)
                    t = DVE.do(lambda e, h=h, q0=q0, ot=ot: e.tensor_tensor(
                        out=mT[:, h, q0:q0 + 256], in0=ot[:, :], in1=mT[:, h, q0:q0 + 256], op=ALU.mult))
                    t_att_dve = t
                    if dbg == "A2" or (dbg == "A3" and m == 3) or (dbg == "A4" and m == 1) or (dbg == "A5" and m == 2):
                        SP.wait(kv_tok[h + 1])
                        dump([mT[:, 2 * i:2 * i + 2, :].rearrange("p a b -> p (a b)") for i in range(NT)],
                             [t_att_dve, tk_last] + r0["out"])
                kv_free[sl] = tk_last
                selT_free[sl] = tk_last
            bank_free[0] = bank_free[1] = t_att_dve
            if dbg == "ATT":
                dump([mT[:, 2 * i:2 * i + 2, :].rearrange("p a b -> p (a b)") for i in range(NT)],
                     [t_att_dve, tk_last] + r0["out"])
            bank_free[2] = bank_free[3] = s_free[0:4]
            tr_free[1] = t_att_dve
            t_att_pe = tk_last

            r1 = x_out(1, x1sp, [t_att_dve], [t_att_pe, t_att_dve], [t_qz_mm_done], [t_att_pe] + r0["out"], [t_att_pe, t_att_dve],
                       final=True)
            final_out_toks = r1["out"]

        fin["toks"] = final_out_toks

    try:
        phases()
    except _Done:
        pass
    final_out_toks = fin["toks"]
    SP.wait(final_out_toks)
    for q in (PE, ACT, DVE, POOL):
        pass

    _check_deadlock([PE, ACT, DVE, POOL, SP])

    with nc.Block() as block:
        @block.sync
        def _(e):
            SP.emit(e)

        @block.gpsimd
        def _(e):
            POOL.emit(e)

        @block.scalar
        def _(e):
            ACT.emit(e)

        @block.vector
        def _(e):
            DVE.emit(e)

        @block.tensor
        def _(e):
            PE.emit(e)
    es.close()
    return nc


def make_consts(rank):
    cst = np.zeros((P, NCST), np.float32)
    s = np.arange(P)[:, None]
    t = np.arange(P)[None, :]
    cst[:, C_M01:C_M01 + 128] = (t >= s).astype(np.float32)
    tri = np.where(s <= t, 0.0, NEG).astype(np.float32)
    if rank == 0:
        cst[:, C_CM0:C_CM0 + 128] = tri
        cst[:, C_CM1:C_CM1 + 128] = NEG
    else:
        cst[:, C_CM0:C_CM0 + 128] = 0.0
        cst[:, C_CM1:C_CM1 + 128] = tri
    cst[:, C_ID:C_ID + 128] = np.eye(P, dtype=np.float32)
    for ii in range(4):
        for j in range(8):
            cst[:, C_FB + ii * 8 + j] = NEG if j >= ii + 4 else 0.0
            cst[:, C_NS + ii * 8 + j] = 1.0 if j < ii + 4 else 0.0
    cste = np.zeros((P, 8, P), np.float32)
    for j in range(8):
        cste[j, j, :] = 1.0
    return cst, cste.reshape(P, 1024)


def tok_index(rank):
    return np.concatenate([np.arange(i * 256 + rank * 128, i * 256 + rank * 128 + 128) for i in range(NT)])


_NC_CACHE = {}


def kernel(x, p, w_in_a, sgu_norm_g, sgu_norm_b, w_s, b_s, w_in_b, w_out, ln_g, ln_b, w_ple_gate, w_ple_proj,
           _dbg=None):
    f = lambda a: np.ascontiguousarray(np.asarray(a, dtype=np.float32))
    x, p = f(x), f(p)
    shared = {
        "w_in_a": f(w_in_a[0]), "w_in_b": f(w_in_b[0]), "w_out": f(w_out), "w_gate": f(w_ple_gate),
        "w_proj": f(w_ple_proj), "sgu_g": f(sgu_norm_g[0:1]), "sgu_b": f(sgu_norm_b[0:1]),
        "wsT": f(np.transpose(np.asarray(w_s[0]), (2, 0, 1))), "bs": f(np.asarray(b_s[0]).reshape(1, D)),
        "ln_g": f(ln_g), "ln_b": f(ln_b),
    }
    in_maps = []
    for c in range(NCORES):
        b, r = c // 2, c % 2
        idx = tok_index(r)
        xc = x[b][idx]
        cst, cste = make_consts(r)
        m = dict(shared)
        m["x"] = f(xc)
        m["xT"] = f(xc.T)
        m["pT"] = f(np.stack([p[l, b][idx].T for l in range(2)]))
        m["cst"] = cst
        m["cste"] = cste
        in_maps.append(m)
    if _dbg not in _NC_CACHE:
        _NC_CACHE[_dbg] = build_nc(_dbg)
    nc = _NC_CACHE[_dbg]
    res = run_bass_kernel_spmd(nc, in_maps, core_ids=list(range(NCORES)))
    outp = np.empty((4, 2048, D), np.float32)
    for c in range(NCORES):
        b, r = c // 2, c % 2
        outp[b][tok_index(r)] = res.results[c]["out"]
    return outp
```
